# Optimizing a Trainium2 kernel written in Bass

```python
import jax, jax.numpy as jnp
from jax import lax
import numpy as np

D_MODEL = 2048
BATCH = 2
SEQ = 8192
DEPTH = 1

D_CONV = D_MODEL // 2
CONV_GROUPS = 8
CONV_K = 31
D_SGU = D_MODEL // 2
SGU_GROUPS = 8
SGU_HEAD = D_SGU // SGU_GROUPS
CHUNK = 128
D_FF = 5632
FFN_K = 3
N_MOD = 6
D_IN = 2 * D_CONV + 2 * D_SGU + 2 * D_MODEL
EPS = 1e-6

kernel_name = "hybrid_conformer_gmlp_convffn_adaln"


def rms_norm(x, g):
    xf = x.astype(jnp.float32)
    y = xf * lax.rsqrt(jnp.mean(xf * xf, axis=-1, keepdims=True) + EPS)
    return (y * g.astype(jnp.float32)).astype(x.dtype)


def layer_norm(x, g, b):
    xf = x.astype(jnp.float32)
    mu = jnp.mean(xf, axis=-1, keepdims=True)
    var = jnp.mean(jnp.square(xf - mu), axis=-1, keepdims=True)
    y = (xf - mu) * lax.rsqrt(var + EPS)
    return (y * g.astype(jnp.float32) + b.astype(jnp.float32)).astype(x.dtype)


def causal_dwconv(x, w, b):
    k, ch = w.shape
    y = lax.conv_general_dilated(
        x, w.astype(x.dtype)[:, None, :], window_strides=(1,), padding=[(k - 1, 0)],
        dimension_numbers=("NWC", "WIO", "NWC"), feature_group_count=ch)
    return y + b.astype(x.dtype)


def setup_inputs(seed: int = 0) -> dict:
    key = jax.random.key(seed)
    ks = jax.random.split(key, 24)
    L, D = DEPTH, D_MODEL
    n = lambda k, shape, s: jax.random.normal(k, shape, jnp.float32) * s
    return {
        "x": n(ks[0], (BATCH, SEQ, D), 1.0),
        "c": n(ks[1], (BATCH, D), 1.0),
        "w_ada": n(ks[2], (L, D, N_MOD * D), 0.5 * D ** -0.5),
        "b_ada": n(ks[3], (L, N_MOD * D), 0.01),
        "norm1_g": 1.0 + n(ks[4], (L, D), 0.02),
        "w_in": n(ks[5], (L, D, D_IN), D ** -0.5),
        "b_in": n(ks[6], (L, D_IN), 0.01),
        "conv_dw_w": n(ks[7], (L, CONV_K, D_CONV), CONV_K ** -0.5),
        "conv_dw_b": n(ks[8], (L, D_CONV), 0.01),
        "conv_ln_g": 1.0 + n(ks[9], (L, D_CONV), 0.02),
        "conv_ln_b": n(ks[10], (L, D_CONV), 0.01),
        "w_conv_out": n(ks[11], (L, D_CONV, D), D_CONV ** -0.5),
        "sgu_ln_g": 1.0 + n(ks[12], (L, D_SGU), 0.02),
        "sgu_ln_b": n(ks[13], (L, D_SGU), 0.01),
        "w_spatial": n(ks[14], (L, SGU_GROUPS, CHUNK, CHUNK), CHUNK ** -0.5),
        "b_spatial": 1.0 + n(ks[15], (L, SGU_GROUPS, CHUNK), 0.01),
        "w_sgu_out": n(ks[16], (L, D_SGU, D), D_SGU ** -0.5),
        "w_out": n(ks[17], (L, D, D), D ** -0.5),
        "norm2_g": 1.0 + n(ks[18], (L, D), 0.02),
        "w_up": n(ks[19], (L, D, 2 * D_FF), D ** -0.5),
        "ffn_dw_w": n(ks[20], (L, FFN_K, 2 * D_FF), FFN_K ** -0.5),
        "ffn_dw_b": n(ks[21], (L, 2 * D_FF), 0.01),
        "w_down": n(ks[22], (L, D_FF, D), D_FF ** -0.5),
        "final_g": 1.0 + n(ks[23], (D,), 0.02),
    }


def reference(x, c, w_ada, b_ada, norm1_g, w_in, b_in, conv_dw_w, conv_dw_b, conv_ln_g,
              conv_ln_b, w_conv_out, sgu_ln_g, sgu_ln_b, w_spatial, b_spatial, w_sgu_out,
              w_out, norm2_g, w_up, ffn_dw_w, ffn_dw_b, w_down, final_g):
    B, S, D = x.shape
    n_chunks = S // CHUNK
    causal_mask = jnp.tril(jnp.ones((CHUNK, CHUNK), dtype=x.dtype))
    c_act = jax.nn.silu(c)
    for l in range(DEPTH):
        mod = (c_act @ w_ada[l] + b_ada[l])[:, None, :]
        shift1, scale1, gate1, shift2, scale2, gate2 = jnp.split(mod, N_MOD, axis=-1)

        h = rms_norm(x, norm1_g[l]) * (1.0 + scale1) + shift1
        proj = h @ w_in[l] + b_in[l]
        a_in, s_in, gates = jnp.split(proj, [2 * D_CONV, 2 * D_CONV + 2 * D_SGU], axis=-1)

        a_val, a_gate = jnp.split(a_in, 2, axis=-1)
        a = a_val * jax.nn.sigmoid(a_gate)
        a = causal_dwconv(a, conv_dw_w[l], conv_dw_b[l])
        a = jax.nn.silu(layer_norm(a, conv_ln_g[l], conv_ln_b[l]))
        y_a = a @ w_conv_out[l]

        z = jax.nn.gelu(s_in, approximate=False)
        u, v = jnp.split(z, 2, axis=-1)
        v = layer_norm(v, sgu_ln_g[l], sgu_ln_b[l])
        v = v.reshape(B, n_chunks, CHUNK, SGU_GROUPS, SGU_HEAD)
        ws = w_spatial[l] * causal_mask
        v = jnp.einsum("gts,bnsgc->bntgc", ws, v) + b_spatial[l].T[:, :, None]
        y_b = (u * v.reshape(B, S, D_SGU)) @ w_sgu_out[l]

        g_a, g_b = jnp.split(gates, 2, axis=-1)
        merged = jax.nn.sigmoid(g_a) * y_a + jax.nn.sigmoid(g_b) * y_b
        x = x + gate1 * (merged @ w_out[l])

        h = rms_norm(x, norm2_g[l]) * (1.0 + scale2) + shift2
        up = causal_dwconv(h @ w_up[l], ffn_dw_w[l], ffn_dw_b[l])
        val, gt = jnp.split(up, 2, axis=-1)
        x = x + gate2 * ((jax.nn.silu(gt) * val) @ w_down[l])

    return rms_norm(x, final_g)
```

```python
import numpy as np
from contextlib import ExitStack
import concourse.bass as bass
import concourse.mybir as mybir
from concourse.bass_utils import run_bass_kernel_spmd

F32 = mybir.dt.float32
BF16 = mybir.dt.bfloat16
AF = mybir.ActivationFunctionType
ALU = mybir.AluOpType

D = 2048
KC = 16
DC = 1024
CC = 8
CONV_K = 31
HALO = CONV_K - 1
EPS = 1e-6
NCORES = 8
SLOT_E = 4096


class Cfg:
    def __init__(self, dff=5632, tiles=(128, 512, 512, 512, 512), nslot=5, ntmp=6):
        self.DFF = dff
        self.FC = dff // 128
        self.HK = self.FC // 2
        self.tiles = tuple(tiles)
        self.TM = max(tiles)
        self.NTOK = sum(tiles)
        self.NOUT = self.NTOK - 128
        self.nslot = nslot
        self.ntmp = ntmp
        self.debug = False


CFG = Cfg()


def vec_layout(cfg):
    FC = cfg.FC
    items = [("n1g", KC), ("n2g", KC), ("fg", KC), ("b_aval", CC), ("b_agate", CC), ("b_u", CC),
             ("b_ga", KC), ("b_gb", KC), ("cdw", CC * CONV_K), ("cdb", CC), ("clg", CC), ("clb", CC),
             ("slg", CC), ("slb", CC), ("fdw", 2 * FC * 3), ("fdb", 2 * FC), ("bada", 96),
             ("flag", 1), ("cT", KC)]
    off = {}
    o = 0
    for name, n in items:
        off[name] = (o, n)
        o += n
    return off, o


class Prog:
    ENG = ("pe", "act", "dve", "pool", "sp")

    def __init__(self):
        self.q = {e: [] for e in self.ENG}
        self.cnt = {e: 0 for e in self.ENG}
        self.waited = {e: {} for e in self.ENG}
        self.lastw = {}
        self.readers = {}
        self.dcnt = {}

    def op(self, eng, fn, reads=(), writes=(), dma=None):
        deps = {}
        for r in reads:
            t = self.lastw.get(r)
            if t is not None:
                deps[t] = True
        for w in writes:
            t = self.lastw.get(w)
            if t is not None:
                deps.setdefault(t, False)
            for t in self.readers.get(w, ()):
                deps.setdefault(t, False)
        if dma is not None:
            n = self.dcnt.get(dma, 0) + 1
            self.dcnt[dma] = n
            tok = ("d_" + dma, 16 * n, None)
        else:
            self.cnt[eng] += 1
            tok = ("e_" + eng, self.cnt[eng], eng)
        wd = self.waited[eng]
        need = {}
        for (sem, val, te), raw in deps.items():
            if te == eng and eng == "pe":
                continue
            if wd.get(sem, 0) >= val:
                continue
            if need.get(sem, 0) < val:
                need[sem] = val
        for sem, val in need.items():
            wd[sem] = val
        self.q[eng].append((list(need.items()), fn, (tok[0], 16 if dma is not None else 1)))
        for r in reads:
            self.readers.setdefault(r, []).append(tok)
        for w in writes:
            self.lastw[w] = tok
            self.readers[w] = []
        return tok

    def sem_names(self):
        s = set()
        for e in self.ENG:
            for waits, _, inc in self.q[e]:
                s.add(inc[0])
                for sem, _ in waits:
                    s.add(sem)
        return sorted(s)

    def emit(self, eng, handle, sems):
        for waits, fn, inc in self.q[eng]:
            for sem, val in waits:
                handle.wait_ge(sems[sem], val)
            ins = fn(handle)
            ins.then_inc(sems[inc[0]], inc[1])


def build_program(cfg):
    FC, HK, TM, NTOK, NOUT = cfg.FC, cfg.HK, cfg.TM, cfg.NTOK, cfg.NOUT
    nc = bass.Bass("TRN2", target_bir_lowering=False)
    P = Prog()
    voff, NV = vec_layout(cfg)

    def din(name, shape):
        return nc.dram_tensor(name, list(shape), F32, kind="ExternalInput").ap()

    xd = din("xT", [128, KC, NTOK])
    vecs_d = din("vecs", [128, NV])
    bvbc_d = din("bvbc", [128, 1024])
    bsp_d = din("bsp", [128, 1024])
    wsp_d = din("wspT", [128, 1024])
    mask_d = din("maskT", [128, 128])
    w_ada_d = din("w_ada", [48, 128, SLOT_E])
    w_in_d = din("w_in", [32, 128, SLOT_E])
    w_co_d = din("w_co", [4, 128, SLOT_E])
    w_so_d = din("w_so", [4, 128, SLOT_E])
    w_out_d = din("w_out", [8, 128, SLOT_E])
    w_up_d = din("w_up", [FC, 128, SLOT_E])
    w_dn_d = din("w_dn", [32, 128, HK * 128])
    yd = nc.dram_tensor("yT", [128, KC, NOUT], F32, kind="ExternalOutput").ap()

    with ExitStack() as st:
        def sb(name, shape, dt):
            return st.enter_context(nc.sbuf_tensor(name, list(shape), dt))

        xT = sb("xTs", [128, KC, TM], F32)
        hT = sb("hT", [128, KC, TM], BF16)
        sqm = sb("sqm", [128, KC, TM], BF16)
        ub = sb("ub", [128, CC, TM], BF16)
        AW = HALO + TM
        o_a, o_ac, o_a2, o_vn = 0, CC * AW, CC * AW + CC * TM, CC * AW + CC * TM + CC * TM // 2
        RW = max(o_vn + 4 * 1024 // 2, FC * TM // 2)
        R = sb("R", [128, RW], F32)
        abuf = R[:, o_a:o_a + CC * AW].rearrange("p (c t) -> p c t", t=AW)
        acb = R[:, o_ac:o_ac + CC * TM].rearrange("p (c t) -> p c t", t=TM)
        a2 = R[:, o_a2:o_a2 + CC * TM // 2].bitcast(BF16).rearrange("p (c t) -> p c t", t=TM)
        vn = R[:, o_vn:o_vn + 2048].bitcast(BF16).rearrange("p (n c) -> p n c", c=1024)
        gbuf = R[:, 0:FC * TM // 2].bitcast(BF16).rearrange("p (c t) -> p c t", t=TM)
        vall = R[:, o_ac:o_ac + 4096].rearrange("p (n c) -> p n c", c=1024)
        CELL = 256

        def rres(lo, hi):
            return [("R", c) for c in range(lo // CELL, (hi - 1) // CELL + 1)]

        def r_a(j):
            return rres(o_a + j * AW, o_a + (j + 1) * AW)

        def r_ac(j):
            return rres(o_ac + j * TM, o_ac + (j + 1) * TM)

        def r_a2(j):
            return rres(o_a2 + j * TM // 2, o_a2 + (j + 1) * TM // 2)

        def r_vn(n):
            return rres(o_vn + n * 512, o_vn + (n + 1) * 512)

        def r_vall(n):
            return rres(o_ac + n * 1024, o_ac + (n + 1) * 1024)

        def r_g(j):
            return rres(j * TM // 2, (j + 1) * TM // 2)

        slots = [sb(f"slot{i}", [128, SLOT_E], BF16) for i in range(cfg.nslot)]
        tmps = [sb(f"tmp{i}", [128, TM], F32) for i in range(cfg.ntmp)]
        longs = [sb(f"long{i}", [128, TM], F32) for i in range(3)]
        vecs = sb("vecs_s", [128, NV], F32)
        dv = sb("dv", [128, 96 + 2 * KC], F32)
        bvbc = sb("bvbc_s", [128, 1024], F32)
        bfull = sb("bfull", [128, CC, 128], F32)
        maskT = sb("mask_s", [128, 128], F32)
        wsb = sb("wsb", [128, CC, 128], BF16)
        onesD = sb("onesD", [128, 128], BF16)
        onesC = sb("onesC", [128, 128], BF16)
        ones1 = sb("ones1", [128, 128], BF16)
        csb = sb("csb", [128, KC], BF16)
        cw = sb("cw", [128, 2 * FC, 2], F32)
        upc = [sb(f"upc{i}", [128, 2 * FC, 2], F32) for i in range(2)]
        acar = sb("acar", [128, CC, HALO], F32)
        bnst = sb("bnst", [128, 4, 12], F32)
        bnmv = sb("bnmv", [128, 4, 4], F32)
        epsT = sb("epsT", [128, 1], F32)
        banks = [st.enter_context(nc.psum_tensor(f"bank{i}", [128, 512], F32)) for i in range(8)]

        def V(name, j=None, n=1):
            o, ln = voff[name]
            if j is None:
                return vecs[:, o:o + ln]
            return vecs[:, o + j:o + j + n]

        MOD = lambda m: dv[:, m:m + 1]
        G1P = lambda k: dv[:, 96 + k:97 + k]
        G2P = lambda k: dv[:, 112 + k:113 + k]

        state = {"bank": 0, "tmp": 0, "slot": 0}

        def dbg(name, ap, reads):
            if not cfg.debug:
                return
            d = nc.dram_tensor("dbg_" + name, list(ap.shape), ap.dtype, kind="ExternalOutput").ap()
            state["ndbg"] = state.get("ndbg", 0) + 1
            P.op("sp", lambda e: e.dma_start(out=d, in_=ap), reads=reads, dma="dbg%d" % state["ndbg"])

        def bank():
            b = state["bank"]
            state["bank"] = (b + 1) % 8
            return banks[b], ("ps", b)

        def tmp():
            i = state["tmp"]
            state["tmp"] = (i + 1) % cfg.ntmp
            return tmps[i], ("tmp", i)

        def load_block(src_ap, nelem=SLOT_E):
            s = state["slot"]
            state["slot"] = (s + 1) % cfg.nslot
            dst = slots[s][:, 0:nelem]
            P.op("pool", lambda e, dst=dst, src=src_ap: e.dma_start(out=dst, in_=src),
                 writes=[("slot", s)], dma=f"slot{s}")
            return slots[s], ("slot", s)

        def mm_group(psap, pairs, reads, bankres, n_extra_writes=()):
            def fn(e, psap=psap, pairs=pairs):
                last = None
                n = len(pairs)
                for i, (l, r) in enumerate(pairs):
                    last = e.matmul(psap, lhsT=l, rhs=r, start=(i == 0), stop=(i == n - 1))
                return last
            P.op("pe", fn, reads=reads, writes=[bankres] + list(n_extra_writes))

        P.op("sp", lambda e: e.dma_start(out=vecs[:], in_=vecs_d), writes=["vecs"], dma="cst0")
        P.op("sp", lambda e: e.dma_start(out=bvbc[:], in_=bvbc_d), writes=["bvbc"], dma="cst1")
        P.op("sp", lambda e: e.dma_start(out=maskT[:], in_=mask_d), writes=["mask"], dma="cst2")
        P.op("sp", lambda e: e.dma_start(out=vall[:, 0, :], in_=wsp_d), writes=r_vall(0), dma="cst3")
        P.op("sp", lambda e: e.dma_start(out=vall[:, 1, :], in_=bsp_d), writes=r_vall(1), dma="cst4")
        P.op("dve", lambda e: e.memset(epsT[:], EPS), writes=["epsT"])
        P.op("dve", lambda e: e.memset(onesD[:], 1.0 / D), writes=["onesD"])
        P.op("dve", lambda e: e.memset(onesC[:], 1.0 / DC), writes=["onesC"])
        P.op("dve", lambda e: e.memset(ones1[:], 1.0), writes=["ones1"])
        P.op("dve", lambda e: e.memset(acar[:], 0.0), writes=["acar"])
        P.op("dve", lambda e: e.memset(upc[0][:], 0.0), writes=[("upc", 0)])
        P.op("act", lambda e: e.activation(out=csb[:], in_=V("cT"), func=AF.Silu), reads=["vecs"], writes=["csb"])
        for g in range(CC):
            P.op("dve", lambda e, g=g: e.tensor_tensor(out=wsb[:, g, :], in0=vall[:, 0, g * 128:(g + 1) * 128],
                                                        in1=maskT[:], op=ALU.mult),
                 reads=r_vall(0) + ["mask"], writes=[("wsb", g)])
        for half in range(2):
            bk, bres = bank()
            for gg in range(4):
                g = half * 4 + gg
                mm_group(bk[:, gg * 128:(gg + 1) * 128], [(ones1[:], wsb[:, g, :])],
                         reads=["ones1", ("wsb", g)], bankres=bres)
            for gg in range(4):
                g = half * 4 + gg
                P.op("dve", lambda e, g=g, gg=gg, bk=bk: e.scalar_tensor_tensor(
                    out=bfull[:, g, :], in0=bk[:, gg * 128:(gg + 1) * 128], scalar=V("slb", g),
                    in1=vall[:, 1, g * 128:(g + 1) * 128], op0=ALU.mult, op1=ALU.add),
                    reads=["vecs"] + r_vall(1), writes=[bres, ("bfull", g)])
        bk_mod, bres_mod = bank()
        for b in range(48):
            sl, sres = load_block(w_ada_d[b])
            wv = sl[:, :].rearrange("p (k c) -> p k c", c=256)
            for mm in range(2):
                m = 2 * b + mm
                mm_group(bk_mod[:, m:m + 1],
                         [(wv[:, k, mm * 128:(mm + 1) * 128], csb[:, k:k + 1]) for k in range(KC)],
                         reads=[sres, "csb"], bankres=bres_mod)
        P.op("dve", lambda e: e.tensor_tensor(out=dv[:, 0:96], in0=bk_mod[:, 0:96], in1=V("bada"), op=ALU.add),
             reads=["vecs"], writes=[bres_mod, "mod"])
        P.op("dve", lambda e: e.scalar_tensor_tensor(out=dv[:, 96:112], in0=dv[:, 16:32], scalar=1.0,
                                                      in1=V("n1g"), op0=ALU.add, op1=ALU.mult),
             reads=["mod", "vecs"], writes=["g1p"])
        P.op("dve", lambda e: e.scalar_tensor_tensor(out=dv[:, 112:128], in0=dv[:, 64:80], scalar=1.0,
                                                      in1=V("n2g"), op0=ALU.add, op1=ALU.mult),
             reads=["mod", "vecs"], writes=["g2p"])
        VEC = ["vecs", "mod", "g1p", "g2p"]

        def rms_to_h(T, gfn, shift_m0, dname=None):
            for k in range(KC):
                P.op("act", lambda e, k=k: e.activation(out=sqm[:, k, :T], in_=xT[:, k, :T], func=AF.Square),
                     reads=[("x", k)], writes=[("sqm", k)])
            bk, bres = bank()
            mm_group(bk[:, :T], [(onesD[:], sqm[:, k, :T]) for k in range(KC)],
                     reads=["onesD"] + [("sqm", k) for k in range(KC)], bankres=bres)
            rstd, rres_ = longs[0], ("L", 0)
            P.op("act", lambda e: e.activation(out=rstd[:, :T], in_=bk[:, :T], func=AF.Sqrt, bias=epsT[:], scale=1.0),
                 reads=["epsT"], writes=[bres, rres_])
            P.op("dve", lambda e: e.reciprocal(out=rstd[:, :T], in_=rstd[:, :T]), reads=[rres_], writes=[rres_])
            if dname is not None:
                dbg(dname + "_sq", sqm[:, :, :T], [("sqm", k) for k in range(KC)])
                dbg(dname + "_rstd", rstd[:, :T], [rres_])
                dbg(dname + "_x", xT[:, :, :T], [("x", k) for k in range(KC)])
            for k in range(KC):
                if gfn is None:
                    continue
                t_, tres = tmp()
                P.op("dve", lambda e, k=k, t_=t_: e.scalar_tensor_tensor(
                    out=t_[:, :T], in0=xT[:, k, :T], scalar=gfn(k), in1=rstd[:, :T], op0=ALU.mult, op1=ALU.mult),
                    reads=[("x", k), rres_] + VEC, writes=[tres])
                P.op("act", lambda e, k=k, t_=t_: e.activation(out=hT[:, k, :T], in_=t_[:, :T], func=AF.Identity,
                                                               bias=MOD(shift_m0 + k), scale=1.0),
                     reads=[tres] + VEC, writes=[("h", k)])
            return rstd, rres_

        HALL = [("h", k) for k in range(KC)]

        def wv256(sl):
            return sl[:, :].rearrange("p (k c) -> p k c", c=256)

        def wv512(sl):
            return sl[:, :].rearrange("p (k c) -> p k c", c=512)

        def do_tile(ti, T, t0):
            halo = (ti == 0)
            nT = T // 128
            cur, nxt = ti % 2, (ti + 1) % 2
            P.op("sp", lambda e, t0=t0, T=T: e.dma_start(out=xT[:, :, :T], in_=xd[:, :, t0:t0 + T]),
                 writes=[("x", k) for k in range(KC)], dma="xin")
            rms_to_h(T, G1P, 0, "n1" if ti == 1 else None)

            if ti == 1:
                dbg("mod", dv[:, :], VEC)
                dbg("h", hT[:, :, :T], HALL)
            P.op("dve", lambda e: e.tensor_copy(out=abuf[:, :, 0:HALO], in_=acar[:]),
                 reads=["acar"], writes=sum([r_a(j) for j in range(CC)], []))
            for jp in range(4):
                slv, rv = load_block(w_in_d[jp])
                slg, rg = load_block(w_in_d[4 + jp])
                for jj in range(2):
                    j = 2 * jp + jj
                    bkv, brv = bank()
                    bkg, brg = bank()
                    mm_group(bkv[:, :T], [(wv256(slv)[:, k, jj * 128:(jj + 1) * 128], hT[:, k, :T]) for k in range(KC)],
                             reads=[rv] + HALL, bankres=brv)
                    mm_group(bkg[:, :T], [(wv256(slg)[:, k, jj * 128:(jj + 1) * 128], hT[:, k, :T]) for k in range(KC)],
                             reads=[rg] + HALL, bankres=brg)
                    sg, sgr = tmp()
                    P.op("act", lambda e, j=j, bkg=bkg, sg=sg: e.activation(
                        out=sg[:, :T], in_=bkg[:, :T], func=AF.Sigmoid, bias=V("b_agate", j), scale=1.0),
                        reads=VEC, writes=[brg, sgr])
                    P.op("dve", lambda e, j=j, bkv=bkv, sg=sg: e.scalar_tensor_tensor(
                        out=abuf[:, j, HALO:HALO + T], in0=bkv[:, :T], scalar=V("b_aval", j), in1=sg[:, :T],
                        op0=ALU.add, op1=ALU.mult), reads=VEC + [sgr], writes=[brv] + r_a(j))

            conv_fill = []

            def conv_chunk(j, T=T, halo=halo):
                P.op("dve", lambda e: e.tensor_scalar(out=acb[:, j, :T], in0=abuf[:, j, 0:T], scalar1=V("cdw", j * CONV_K),
                                                      scalar2=V("cdb", j), op0=ALU.mult, op1=ALU.add),
                     reads=VEC + r_a(j), writes=r_ac(j))
                for k in range(1, CONV_K):
                    P.op("dve", lambda e, k=k: e.scalar_tensor_tensor(
                        out=acb[:, j, :T], in0=abuf[:, j, k:k + T], scalar=V("cdw", j * CONV_K + k),
                        in1=acb[:, j, :T], op0=ALU.mult, op1=ALU.add),
                        reads=VEC + r_a(j) + r_ac(j), writes=r_ac(j))
                if halo:
                    P.op("dve", lambda e: e.tensor_scalar(out=acar[:, j, :], in0=abuf[:, j, T:T + HALO],
                                                          scalar1=V("flag", 0), scalar2=None, op0=ALU.mult),
                         reads=VEC + r_a(j), writes=["acar"])
                else:
                    P.op("dve", lambda e: e.tensor_copy(out=acar[:, j, :], in_=abuf[:, j, T:T + HALO]),
                         reads=r_a(j), writes=["acar"])

            for j in range(CC):
                conv_fill.append(lambda j=j: conv_chunk(j))

            def fill(n=1):
                for _ in range(n):
                    if conv_fill:
                        conv_fill.pop(0)()

            vblk = [load_block(w_in_d[12 + i]) for i in range(4)]
            for n in range(nT):
                bkA, brA = bank()
                bkB, brB = bank()
                for (bk_, br_, b0) in ((bkA, brA, 0), (bkB, brB, 2)):
                    for bb in range(2):
                        sl, sres = vblk[b0 + bb]
                        mm_group(bk_[:, bb * 256:(bb + 1) * 256],
                                 [(hT[:, k, n * 128:(n + 1) * 128], wv256(sl)[:, k, :]) for k in range(KC)],
                                 reads=[sres] + HALL, bankres=br_)
                P.op("dve", lambda e, n=n, bkA=bkA: e.tensor_tensor(out=vall[:, n, 0:512], in0=bkA[:, :],
                                                                    in1=bvbc[:, 0:512], op=ALU.add),
                     reads=["bvbc"], writes=[brA] + r_vall(n))
                P.op("dve", lambda e, n=n, bkB=bkB: e.tensor_tensor(out=vall[:, n, 512:1024], in0=bkB[:, :],
                                                                    in1=bvbc[:, 512:1024], op=ALU.add),
                     reads=["bvbc"], writes=[brB] + r_vall(n))
                P.op("act", lambda e, n=n: e.activation(out=vall[:, n, :], in_=vall[:, n, :], func=AF.Gelu),
                     reads=r_vall(n), writes=r_vall(n))
                P.op("dve", lambda e, n=n: e.bn_stats(out=bnst[:, n, 0:6], in_=vall[:, n, 0:512]),
                     reads=r_vall(n), writes=["bnst"])
                P.op("dve", lambda e, n=n: e.bn_stats(out=bnst[:, n, 6:12], in_=vall[:, n, 512:1024]),
                     reads=r_vall(n), writes=["bnst"])
                P.op("dve", lambda e, n=n: e.bn_aggr(out=bnmv[:, n, 0:2], in_=bnst[:, n, :]),
                     reads=["bnst"], writes=["bnmv"])
            P.op("act", lambda e, nT=nT: e.activation(out=bnmv[:, 0:nT, 2], in_=bnmv[:, 0:nT, 1], func=AF.Sqrt,
                                                      bias=epsT[:], scale=1.0),
                 reads=["bnmv", "epsT"], writes=["bnmv"])
            P.op("dve", lambda e, nT=nT: e.reciprocal(out=bnmv[:, 0:nT, 2], in_=bnmv[:, 0:nT, 2]),
                 reads=["bnmv"], writes=["bnmv"])
            for n in range(nT):
                P.op("dve", lambda e, n=n: e.tensor_scalar(out=vn[:, n, :], in0=vall[:, n, :],
                                                           scalar1=bnmv[:, n, 0:1], scalar2=bnmv[:, n, 2:3],
                                                           op0=ALU.subtract, op1=ALU.mult),
                     reads=r_vall(n) + ["bnmv"], writes=r_vn(n))

            for gp in range(4):
                slu, ru = load_block(w_in_d[8 + gp])
                for gg in range(2):
                    g = 2 * gp + gg
                    bku, bru = bank()
                    bks, brs = bank()
                    mm_group(bku[:, :T], [(wv256(slu)[:, k, gg * 128:(gg + 1) * 128], hT[:, k, :T]) for k in range(KC)],
                             reads=[ru] + HALL, bankres=bru)
                    for n in range(nT):
                        mm_group(bks[:, n * 128:(n + 1) * 128], [(vn[:, n, g * 128:(g + 1) * 128], wsb[:, g, :])],
                                 reads=[("wsb", g)] + r_vn(n), bankres=brs)
                    gu, gur = tmp()
                    P.op("act", lambda e, g=g, bku=bku, gu=gu: e.activation(
                        out=gu[:, :T], in_=bku[:, :T], func=AF.Gelu, bias=V("b_u", g), scale=1.0),
                        reads=VEC, writes=[bru, gur])
                    t2, t2r = tmp()
                    for n in range(nT):
                        P.op("dve", lambda e, g=g, n=n, bks=bks, t2=t2: e.scalar_tensor_tensor(
                            out=t2[:, n * 128:(n + 1) * 128], in0=bks[:, n * 128:(n + 1) * 128], scalar=V("slg", g),
                            in1=bfull[:, g, :], op0=ALU.mult, op1=ALU.add),
                            reads=VEC + [("bfull", g)], writes=[brs, t2r])
                    P.op("dve", lambda e, g=g, t2=t2, gu=gu: e.tensor_tensor(out=ub[:, g, :T], in0=t2[:, :T],
                                                                            in1=gu[:, :T], op=ALU.mult),
                         reads=[t2r, gur], writes=[("ub", g)])
                    fill(1)
            fill(CC)

            if ti == 1:
                dbg("abuf", abuf[:, :, 0:HALO + T], sum([r_a(j) for j in range(CC)], []))
                dbg("vn", vn[:, 0:nT, :], sum([r_vn(n) for n in range(nT)], []))
                dbg("ub", ub[:, :, :T], [("ub", k) for k in range(CC)])
                dbg("ac", acb[:, :, :T], sum([r_ac(j) for j in range(CC)], []))
            for j in range(CC):
                P.op("act", lambda e, j=j: e.activation(out=sqm[:, j, :T], in_=acb[:, j, :T], func=AF.Identity),
                     reads=r_ac(j), writes=[("sqm", j)])
                P.op("act", lambda e, j=j: e.activation(out=sqm[:, CC + j, :T], in_=acb[:, j, :T], func=AF.Square),
                     reads=r_ac(j), writes=[("sqm", CC + j)])
            bkm, brm = bank()
            bkq, brq = bank()
            mm_group(bkm[:, :T], [(onesC[:], sqm[:, j, :T]) for j in range(CC)],
                     reads=["onesC"] + [("sqm", j) for j in range(CC)], bankres=brm)
            mm_group(bkq[:, :T], [(onesC[:], sqm[:, CC + j, :T]) for j in range(CC)],
                     reads=["onesC"] + [("sqm", CC + j) for j in range(CC)], bankres=brq)
            mean, meanr = longs[1], ("L", 1)
            P.op("act", lambda e: e.activation(out=mean[:, :T], in_=bkm[:, :T], func=AF.Identity),
                 writes=[brm, meanr])
            m2, m2r = longs[2], ("L", 2)
            P.op("dve", lambda e: e.tensor_tensor(out=m2[:, :T], in0=mean[:, :T], in1=mean[:, :T], op=ALU.mult),
                 reads=[meanr], writes=[m2r])
            P.op("dve", lambda e: e.tensor_tensor(out=m2[:, :T], in0=bkq[:, :T], in1=m2[:, :T], op=ALU.subtract),
                 reads=[m2r], writes=[brq, m2r])
            P.op("act", lambda e: e.activation(out=m2[:, :T], in_=m2[:, :T], func=AF.Sqrt, bias=epsT[:], scale=1.0),
                 reads=[m2r, "epsT"], writes=[m2r])
            P.op("dve", lambda e: e.reciprocal(out=m2[:, :T], in_=m2[:, :T]), reads=[m2r], writes=[m2r])
            for j in range(CC):
                lt, ltr = tmp()
                P.op("dve", lambda e, j=j, lt=lt: e.tensor_tensor(out=lt[:, :T], in0=acb[:, j, :T], in1=mean[:, :T],
                                                                  op=ALU.subtract),
                     reads=r_ac(j) + [meanr], writes=[ltr])
                P.op("dve", lambda e, lt=lt: e.tensor_tensor(out=lt[:, :T], in0=lt[:, :T], in1=m2[:, :T], op=ALU.mult),
                     reads=[ltr, m2r], writes=[ltr])
                P.op("act", lambda e, j=j, lt=lt: e.activation(out=a2[:, j, :T], in_=lt[:, :T], func=AF.Silu,
                                                               bias=V("clb", j), scale=V("clg", j)),
                     reads=[ltr] + VEC, writes=r_a2(j))

            cur_blocks = {}
            for j in range(KC):
                if j % 2 == 0:
                    cur_blocks["ga"] = load_block(w_in_d[16 + j // 2])
                    cur_blocks["gb"] = load_block(w_in_d[24 + j // 2])
                if j % 4 == 0:
                    cur_blocks["so"] = load_block(w_so_d[j // 4])
                    cur_blocks["co"] = load_block(w_co_d[j // 4])
                (sga_, rga), (sgb_, rgb) = cur_blocks["ga"], cur_blocks["gb"]
                (sso, rso), (sco, rco) = cur_blocks["so"], cur_blocks["co"]
                jj = j % 2
                j4 = j % 4
                bkga, brga = bank()
                bkgb, brgb = bank()
                bkyb, bryb = bank()
                bkya, brya = bank()
                mm_group(bkga[:, :T], [(wv256(sga_)[:, k, jj * 128:(jj + 1) * 128], hT[:, k, :T]) for k in range(KC)],
                         reads=[rga] + HALL, bankres=brga)
                mm_group(bkgb[:, :T], [(wv256(sgb_)[:, k, jj * 128:(jj + 1) * 128], hT[:, k, :T]) for k in range(KC)],
                         reads=[rgb] + HALL, bankres=brgb)
                mm_group(bkyb[:, :T], [(wv512(sso)[:, k, j4 * 128:(j4 + 1) * 128], ub[:, k, :T]) for k in range(CC)],
                         reads=[rso] + [("ub", k) for k in range(CC)], bankres=bryb)
                mm_group(bkya[:, :T], [(wv512(sco)[:, k, j4 * 128:(j4 + 1) * 128], a2[:, k, :T]) for k in range(CC)],
                         reads=[rco] + sum([r_a2(k) for k in range(CC)], []), bankres=brya)
                sa, sar = tmp()
                sb_, sbr = tmp()
                P.op("act", lambda e, j=j, bkga=bkga, sa=sa: e.activation(
                    out=sa[:, :T], in_=bkga[:, :T], func=AF.Sigmoid, bias=V("b_ga", j), scale=1.0),
                    reads=VEC, writes=[brga, sar])
                P.op("act", lambda e, j=j, bkgb=bkgb, sb_=sb_: e.activation(
                    out=sb_[:, :T], in_=bkgb[:, :T], func=AF.Sigmoid, bias=V("b_gb", j), scale=1.0),
                    reads=VEC, writes=[brgb, sbr])
                P.op("dve", lambda e, bkyb=bkyb, sb_=sb_: e.tensor_tensor(out=sb_[:, :T], in0=sb_[:, :T], in1=bkyb[:, :T],
                                                                          op=ALU.mult),
                     reads=[sbr], writes=[bryb, sbr])
                P.op("dve", lambda e, bkya=bkya, sa=sa: e.tensor_tensor(out=sa[:, :T], in0=sa[:, :T], in1=bkya[:, :T],
                                                                        op=ALU.mult),
                     reads=[sar], writes=[brya, sar])
                P.op("dve", lambda e, j=j, sa=sa, sb_=sb_: e.tensor_tensor(out=sqm[:, j, :T], in0=sa[:, :T], in1=sb_[:, :T],
                                                                           op=ALU.add),
                     reads=[sar, sbr], writes=[("sqm", j)])

            if ti == 1:
                dbg("a2", a2[:, :, :T], sum([r_a2(j) for j in range(CC)], []))
                dbg("merged", sqm[:, :, :T], [("sqm", k) for k in range(KC)])
            for j in range(KC):
                if j % 2 == 0:
                    swo, rwo = load_block(w_out_d[j // 2])
                jj = j % 2
                bk, br = bank()
                mm_group(bk[:, :T], [(wv256(swo)[:, k, jj * 128:(jj + 1) * 128], sqm[:, k, :T]) for k in range(KC)],
                         reads=[rwo] + [("sqm", k) for k in range(KC)], bankres=br)
                P.op("dve", lambda e, j=j, bk=bk: e.scalar_tensor_tensor(
                    out=xT[:, j, :T], in0=bk[:, :T], scalar=MOD(32 + j), in1=xT[:, j, :T], op0=ALU.mult, op1=ALU.add),
                    reads=VEC + [("x", j)], writes=[br, ("x", j)])

            if ti == 1:
                dbg("x1", xT[:, :, :T], [("x", k) for k in range(KC)])
            rms_to_h(T, G2P, 48)

            def fd(jj, k):
                return V("fdw", jj * 3 + k)

            o_fdw = voff["fdw"][0]
            fdw3 = vecs[:, o_fdw:o_fdw + 2 * FC * 3].rearrange("p (j k) -> p j k", k=3)
            if not halo:
                P.op("dve", lambda e, cur=cur: e.tensor_tensor(out=cw[:, :, 0], in0=fdw3[:, :, 0], in1=upc[cur][:, :, 0],
                                                               op=ALU.mult), reads=VEC + [("upc", cur)], writes=["cw"])
                P.op("dve", lambda e, cur=cur: e.tensor_tensor(out=cw[:, :, 1], in0=fdw3[:, :, 1], in1=upc[cur][:, :, 1],
                                                               op=ALU.mult), reads=VEC + [("upc", cur)], writes=["cw"])
                P.op("dve", lambda e: e.tensor_tensor(out=cw[:, :, 0], in0=cw[:, :, 0], in1=cw[:, :, 1], op=ALU.add),
                     reads=["cw"], writes=["cw"])
                P.op("dve", lambda e, cur=cur: e.tensor_tensor(out=cw[:, :, 1], in0=fdw3[:, :, 0], in1=upc[cur][:, :, 1],
                                                               op=ALU.mult), reads=VEC + [("upc", cur)], writes=["cw"])
            for jf in range(FC):
                sup, rup = load_block(w_up_d[jf])
                outs = []
                for half, jj in ((0, jf), (1, FC + jf)):
                    bk, br = bank()
                    mm_group(bk[:, :T], [(wv256(sup)[:, k, half * 128:(half + 1) * 128], hT[:, k, :T]) for k in range(KC)],
                             reads=[rup] + HALL, bankres=br)
                    if halo:
                        P.op("dve", lambda e, jj=jj, bk=bk, nxt=nxt: e.tensor_scalar(
                            out=upc[nxt][:, jj, :], in0=bk[:, T - 2:T], scalar1=V("flag", 0), scalar2=None, op0=ALU.mult),
                            reads=VEC, writes=[br, ("upc", nxt)])
                        continue
                    o_, orr = tmp()
                    P.op("act", lambda e, jj=jj, bk=bk, o_=o_: e.activation(
                        out=o_[:, :T], in_=bk[:, :T], func=AF.Identity, bias=V("fdb", jj), scale=fd(jj, 2)),
                        reads=VEC, writes=[br, orr])
                    P.op("dve", lambda e, jj=jj, bk=bk, o_=o_: e.scalar_tensor_tensor(
                        out=o_[:, 1:T], in0=bk[:, 0:T - 1], scalar=fd(jj, 1), in1=o_[:, 1:T], op0=ALU.mult, op1=ALU.add),
                        reads=VEC + [orr], writes=[br, orr])
                    P.op("dve", lambda e, jj=jj, bk=bk, o_=o_: e.scalar_tensor_tensor(
                        out=o_[:, 2:T], in0=bk[:, 0:T - 2], scalar=fd(jj, 0), in1=o_[:, 2:T], op0=ALU.mult, op1=ALU.add),
                        reads=VEC + [orr], writes=[br, orr])
                    P.op("dve", lambda e, jj=jj, o_=o_: e.tensor_tensor(out=o_[:, 0:2], in0=o_[:, 0:2], in1=cw[:, jj, :],
                                                                        op=ALU.add),
                         reads=["cw", orr], writes=[orr])
                    P.op("dve", lambda e, jj=jj, bk=bk, nxt=nxt: e.tensor_copy(out=upc[nxt][:, jj, :], in_=bk[:, T - 2:T]),
                         writes=[br, ("upc", nxt)])
                    outs.append((o_, orr))
                if halo:
                    continue
                (ov, ovr), (og, ogr) = outs
                P.op("act", lambda e, og=og: e.activation(out=og[:, :T], in_=og[:, :T], func=AF.Silu),
                     reads=[ogr], writes=[ogr])
                P.op("dve", lambda e, jf=jf, ov=ov, og=og: e.tensor_tensor(out=gbuf[:, jf, :T], in0=og[:, :T], in1=ov[:, :T],
                                                                           op=ALU.mult),
                     reads=[ovr, ogr], writes=r_g(jf))

            if ti == 1:
                dbg("g", gbuf[:, :, :T], sum([r_g(k) for k in range(FC)], []))
                dbg("cw", cw[:, :, :], ["cw"])
            if not halo:
                for j in range(KC):
                    sd0, rd0 = load_block(w_dn_d[2 * j], HK * 128)
                    sd1, rd1 = load_block(w_dn_d[2 * j + 1], HK * 128)
                    bk, br = bank()
                    pairs = []
                    for hh, sd in ((0, sd0), (1, sd1)):
                        wv = sd[:, 0:HK * 128].rearrange("p (k c) -> p k c", c=128)
                        for kk in range(HK):
                            pairs.append((wv[:, kk, :], gbuf[:, hh * HK + kk, :T]))
                    mm_group(bk[:, :T], pairs, reads=[rd0, rd1] + sum([r_g(k) for k in range(FC)], []), bankres=br)
                    P.op("dve", lambda e, j=j, bk=bk: e.scalar_tensor_tensor(
                        out=xT[:, j, :T], in0=bk[:, :T], scalar=MOD(80 + j), in1=xT[:, j, :T], op0=ALU.mult, op1=ALU.add),
                        reads=VEC + [("x", j)], writes=[br, ("x", j)])
                if ti == 1:
                    dbg("x2", xT[:, :, :T], [("x", k) for k in range(KC)])
                rstd, rres_ = rms_to_h(T, None, 0)
                for k in range(KC):
                    P.op("dve", lambda e, k=k, rstd=rstd: e.scalar_tensor_tensor(
                        out=xT[:, k, :T], in0=xT[:, k, :T], scalar=V("fg", k), in1=rstd[:, :T], op0=ALU.mult, op1=ALU.mult),
                        reads=VEC + [("x", k), rres_], writes=[("x", k)])
                P.op("sp", lambda e, t0=t0, T=T: e.dma_start(out=yd[:, :, t0 - 128:t0 - 128 + T], in_=xT[:, :, :T]),
                     reads=[("x", k) for k in range(KC)], dma="yout")

        t0_ = 0
        for ti_, T_ in enumerate(cfg.tiles):
            do_tile(ti_, T_, t0_)
            t0_ += T_

        n_out = P.dcnt.get("yout", 0)

        names = P.sem_names()
        sems = {n: st.enter_context(nc.semaphore(n)) for n in names}
        block = st.enter_context(nc.Block())

        @block.tensor
        def _(e):
            P.emit("pe", e, sems)

        @block.scalar
        def _(e):
            P.emit("act", e, sems)

        @block.vector
        def _(e):
            P.emit("dve", e, sems)

        @block.gpsimd
        def _(e):
            P.emit("pool", e, sems)

        @block.sync
        def _(e):
            P.emit("sp", e, sems)
            e.wait_ge(sems["d_yout"], 16 * n_out)
            for i in range(1, state.get("ndbg", 0) + 1):
                e.wait_ge(sems["d_dbg%d" % i], 16)
    return nc


def _blk_k(W, kc, cols):
    K, N = W.shape
    nb = N // cols
    return np.ascontiguousarray(W.reshape(kc, 128, nb, cols).transpose(2, 1, 0, 3).reshape(nb, 128, kc * cols))


def _fm(v, n):
    return np.asarray(v, np.float32).reshape(n, 128).T


def prep_shared(cfg, inp):
    FC, HK, DFF = cfg.FC, cfg.HK, cfg.DFF
    f = lambda a: np.asarray(a, np.float32)
    sh = {}
    sh["w_ada"] = _blk_k(f(inp["w_ada"])[0], KC, 256)
    sh["w_in"] = _blk_k(f(inp["w_in"])[0], KC, 256)
    sh["w_co"] = _blk_k(f(inp["w_conv_out"])[0], CC, 512)
    sh["w_so"] = _blk_k(f(inp["w_sgu_out"])[0], CC, 512)
    sh["w_out"] = _blk_k(f(inp["w_out"])[0], KC, 256)
    wu = f(inp["w_up"])[0]
    wu2 = np.concatenate([wu[:, :DFF].reshape(D, FC, 1, 128), wu[:, DFF:].reshape(D, FC, 1, 128)], axis=2)
    sh["w_up"] = _blk_k(wu2.reshape(D, FC * 256), KC, 256)
    wd = f(inp["w_down"])[0]
    sh["w_dn"] = np.ascontiguousarray(
        wd.reshape(2, HK, 128, KC, 128).transpose(3, 0, 2, 1, 4).reshape(2 * KC, 128, HK * 128))
    b_in = f(inp["b_in"])[0]
    sh["bvbc"] = np.ascontiguousarray(np.broadcast_to(b_in[3072:4096][None, :], (128, 1024)))
    sh["bsp"] = np.ascontiguousarray(np.broadcast_to(f(inp["b_spatial"])[0].reshape(1, 1024), (128, 1024)))
    sh["wspT"] = np.ascontiguousarray(f(inp["w_spatial"])[0].transpose(2, 0, 1).reshape(128, 1024))
    sh["maskT"] = np.ascontiguousarray(np.triu(np.ones((128, 128), np.float32)))
    voff, NV = vec_layout(cfg)
    vecs = np.zeros((128, NV), np.float32)

    def put(name, arr):
        o, n = voff[name]
        assert arr.shape == (128, n), (name, arr.shape, n)
        vecs[:, o:o + n] = arr

    put("n1g", _fm(f(inp["norm1_g"])[0], KC))
    put("n2g", _fm(f(inp["norm2_g"])[0], KC))
    put("fg", _fm(f(inp["final_g"]), KC))
    put("b_aval", _fm(b_in[0:1024], CC))
    put("b_agate", _fm(b_in[1024:2048], CC))
    put("b_u", _fm(b_in[2048:3072], CC))
    put("b_ga", _fm(b_in[4096:6144], KC))
    put("b_gb", _fm(b_in[6144:8192], KC))
    put("cdw", np.ascontiguousarray(f(inp["conv_dw_w"])[0].reshape(CONV_K, CC, 128).transpose(2, 1, 0)).reshape(128, CC * CONV_K))
    put("cdb", _fm(f(inp["conv_dw_b"])[0], CC))
    put("clg", _fm(f(inp["conv_ln_g"])[0], CC))
    put("clb", _fm(f(inp["conv_ln_b"])[0], CC))
    put("slg", _fm(f(inp["sgu_ln_g"])[0], CC))
    put("slb", _fm(f(inp["sgu_ln_b"])[0], CC))
    put("fdw", np.ascontiguousarray(f(inp["ffn_dw_w"])[0].reshape(3, 2 * FC, 128).transpose(2, 1, 0)).reshape(128, 2 * FC * 3))
    put("fdb", _fm(f(inp["ffn_dw_b"])[0], 2 * FC))
    put("bada", _fm(f(inp["b_ada"])[0], 96))
    sh["vecs"] = vecs
    return sh, voff


def make_in_maps(cfg, inp):
    x = np.asarray(inp["x"], np.float32)
    c = np.asarray(inp["c"], np.float32)
    B, S, _ = x.shape
    per_seq = NCORES // B
    own = cfg.NOUT
    assert per_seq * own == S
    sh, voff = prep_shared(cfg, inp)
    in_maps = []
    for core in range(NCORES):
        b, q = divmod(core, per_seq)
        s0 = q * own
        xc = np.zeros((128 + own, D), np.float32)
        xc[128:] = x[b, s0:s0 + own]
        if q > 0:
            xc[:128] = x[b, s0 - 128:s0]
        xT = np.ascontiguousarray(xc.reshape(128 + own, KC, 128).transpose(2, 1, 0))
        vecs = sh["vecs"].copy()
        o, n = voff["flag"]
        vecs[:, o] = 1.0 if q > 0 else 0.0
        o, n = voff["cT"]
        vecs[:, o:o + n] = _fm(c[b], KC)
        m = {k: v for k, v in sh.items() if k != "vecs"}
        m["vecs"] = vecs
        m["xT"] = xT
        in_maps.append(m)
    return in_maps


def kernel(**inp):
    cfg = CFG
    x = np.asarray(inp["x"], np.float32)
    B, S, _ = x.shape
    per_seq = NCORES // B
    own = cfg.NOUT
    in_maps = make_in_maps(cfg, inp)
    nc = build_program(cfg)
    res = run_bass_kernel_spmd(nc, in_maps, core_ids=list(range(NCORES)))
    global LAST_RES
    LAST_RES = res.results
    out = np.empty((B, S, D), np.float32)
    for core in range(NCORES):
        b, q = divmod(core, per_seq)
        yT = np.asarray(res.results[core]["yT"])
        out[b, q * own:(q + 1) * own] = yT.transpose(2, 1, 0).reshape(own, D)
    return out
```

```python
import numpy as np
from contextlib import ExitStack
import concourse.bass as bass
import concourse.mybir as mybir
from concourse.bass_utils import run_bass_kernel_spmd

F32 = mybir.dt.float32
BF16 = mybir.dt.bfloat16
AF = mybir.ActivationFunctionType
ALU = mybir.AluOpType

D = 2048
KC = 16
DC = 1024
CC = 8
CONV_K = 31
HALO = CONV_K - 1
EPS = 1e-6
NCORES = 8
SLOT_E = 4096


class Cfg:
    def __init__(self, dff=5632, tiles=(128, 512, 512, 512, 512), nslot=5, ntmp=6):
        self.DFF = dff
        self.FC = dff // 128
        self.HK = self.FC // 2
        self.tiles = tuple(tiles)
        self.TM = max(tiles)
        self.NTOK = sum(tiles)
        self.NOUT = self.NTOK - 128
        self.nslot = nslot
        self.ntmp = ntmp
        self.debug = False


CFG = Cfg()


def vec_layout(cfg):
    FC = cfg.FC
    items = [("n1g", KC), ("n2g", KC), ("fg", KC), ("b_aval", CC), ("b_agate", CC), ("b_u", CC),
             ("b_ga", KC), ("b_gb", KC), ("cdw", CC * CONV_K), ("cdb", CC), ("clg", CC), ("clb", CC),
             ("slg", CC), ("slb", CC), ("fdw", 2 * FC * 3), ("fdb", 2 * FC), ("bada", 96),
             ("flag", 1), ("cT", KC)]
    off = {}
    o = 0
    for name, n in items:
        off[name] = (o, n)
        o += n
    return off, o


class Prog:
    ENG = ("pe", "act", "dve", "pool", "sp")

    def __init__(self):
        self.q = {e: [] for e in self.ENG}
        self.cnt = {e: 0 for e in self.ENG}
        self.waited = {e: {} for e in self.ENG}
        self.lastw = {}
        self.readers = {}
        self.dcnt = {}

    def op(self, eng, fn, reads=(), writes=(), dma=None):
        deps = {}
        for r in reads:
            t = self.lastw.get(r)
            if t is not None:
                deps[t] = True
        for w in writes:
            t = self.lastw.get(w)
            if t is not None:
                deps.setdefault(t, False)
            for t in self.readers.get(w, ()):
                deps.setdefault(t, False)
        if dma is not None:
            n = self.dcnt.get(dma, 0) + 1
            self.dcnt[dma] = n
            tok = ("d_" + dma, 16 * n, None)
        else:
            self.cnt[eng] += 1
            tok = ("e_" + eng, self.cnt[eng], eng)
        wd = self.waited[eng]
        need = {}
        for (sem, val, te), raw in deps.items():
            if te == eng and eng == "pe":
                continue
            if wd.get(sem, 0) >= val:
                continue
            if need.get(sem, 0) < val:
                need[sem] = val
        for sem, val in need.items():
            wd[sem] = val
        self.q[eng].append((list(need.items()), fn, (tok[0], 16 if dma is not None else 1)))
        for r in reads:
            self.readers.setdefault(r, []).append(tok)
        for w in writes:
            self.lastw[w] = tok
            self.readers[w] = []
        return tok

    def sem_names(self):
        s = set()
        for e in self.ENG:
            for waits, _, inc in self.q[e]:
                s.add(inc[0])
                for sem, _ in waits:
                    s.add(sem)
        return sorted(s)

    def emit(self, eng, handle, sems):
        for waits, fn, inc in self.q[eng]:
            for sem, val in waits:
                handle.wait_ge(sems[sem], val)
            ins = fn(handle)
            ins.then_inc(sems[inc[0]], inc[1])


def build_program(cfg):
    FC, HK, TM, NTOK, NOUT = cfg.FC, cfg.HK, cfg.TM, cfg.NTOK, cfg.NOUT
    nc = bass.Bass("TRN2", target_bir_lowering=False)
    P = Prog()
    voff, NV = vec_layout(cfg)

    def din(name, shape):
        return nc.dram_tensor(name, list(shape), F32, kind="ExternalInput").ap()

    xd = din("xT", [128, KC, NTOK])
    vecs_d = din("vecs", [128, NV])
    bvbc_d = din("bvbc", [128, 1024])
    bsp_d = din("bsp", [128, 1024])
    wsp_d = din("wspT", [128, 1024])
    mask_d = din("maskT", [128, 128])
    ident_d = din("ident", [128, 128])
    w_ada_d = din("w_ada", [48, 128, SLOT_E])
    w_in_d = din("w_in", [32, 128, SLOT_E])
    w_co_d = din("w_co", [4, 128, SLOT_E])
    w_so_d = din("w_so", [4, 128, SLOT_E])
    w_out_d = din("w_out", [8, 128, SLOT_E])
    w_up_d = din("w_up", [FC, 128, SLOT_E])
    w_dn_d = din("w_dn", [32, 128, HK * 128])
    yd = nc.dram_tensor("yT", [128, KC, NOUT], F32, kind="ExternalOutput").ap()

    with ExitStack() as st:
        def sb(name, shape, dt):
            return st.enter_context(nc.sbuf_tensor(name, list(shape), dt))

        xT = sb("xTs", [128, KC, TM], F32)
        hT = sb("hT", [128, KC, TM], BF16)
        sqm = sb("sqm", [128, KC, TM], BF16)
        ub = sb("ub", [128, CC, TM], BF16)
        AW = HALO + TM
        AWW = CC * AW // 2
        o_a, o_ac, o_a2, o_vn = 0, AWW, AWW + CC * TM, AWW + CC * TM + CC * TM // 2
        RW = max(o_vn + 4 * 1024 // 2, FC * TM // 2)
        R = sb("R", [128, RW], F32)
        abuf = R[:, o_a:o_a + AWW].bitcast(BF16).rearrange("p (c t) -> p c t", t=AW)
        acb = R[:, o_ac:o_ac + CC * TM].rearrange("p (c t) -> p c t", t=TM)
        a2 = R[:, o_a2:o_a2 + CC * TM // 2].bitcast(BF16).rearrange("p (c t) -> p c t", t=TM)
        vn = R[:, o_vn:o_vn + 2048].bitcast(BF16).rearrange("p (n c) -> p n c", c=1024)
        gbuf = R[:, 0:FC * TM // 2].bitcast(BF16).rearrange("p (c t) -> p c t", t=TM)
        vall = R[:, o_ac:o_ac + 4096].rearrange("p (n c) -> p n c", c=1024)
        CELL = 256

        def rres(lo, hi):
            return [("R", c) for c in range(lo // CELL, (hi - 1) // CELL + 1)]

        def r_a(j):
            return rres(o_a + j * AW // 2, o_a + (j + 1) * AW // 2)

        def r_ac(j):
            return rres(o_ac + j * TM, o_ac + (j + 1) * TM)

        def r_a2(j):
            return rres(o_a2 + j * TM // 2, o_a2 + (j + 1) * TM // 2)

        def r_vn(n):
            return rres(o_vn + n * 512, o_vn + (n + 1) * 512)

        def r_vall(n):
            return rres(o_ac + n * 1024, o_ac + (n + 1) * 1024)

        def r_g(j):
            return rres(j * TM // 2, (j + 1) * TM // 2)

        slots = [sb(f"slot{i}", [128, SLOT_E], BF16) for i in range(cfg.nslot)]
        tmps = [sb(f"tmp{i}", [128, TM], F32) for i in range(cfg.ntmp)]
        longs = [sb(f"long{i}", [128, TM], F32) for i in range(3)]
        vecs = sb("vecs_s", [128, NV], F32)
        dv = sb("dv", [128, 96 + 2 * KC], F32)
        bvbc = sb("bvbc_s", [128, 1024], F32)
        bfull = sb("bfull", [128, CC, 128], F32)
        maskT = sb("mask_s", [128, 128], F32)
        wsb = sb("wsb", [128, CC, 128], BF16)
        onesD = sb("onesD", [128, 128], BF16)
        onesC = sb("onesC", [128, 128], BF16)
        ones1 = sb("ones1", [128, 128], BF16)
        csb = sb("csb", [128, KC], BF16)
        cw = sb("cw", [128, 2 * FC, 2], F32)
        hhalo = sb("hhalo", [128, KC, 2], BF16)
        upc = [sb(f"upc{i}", [128, 2 * FC, 2], F32) for i in range(2)]
        acar = sb("acar", [128, CC, HALO], BF16)
        identb = sb("identb", [128, 128], BF16)
        dg = sb("dg", [128, CONV_K, 128], BF16)
        bnst = sb("bnst", [128, 4, 12], F32)
        bnmv = sb("bnmv", [128, 4, 4], F32)
        epsT = sb("epsT", [128, 1], F32)
        banks = [st.enter_context(nc.psum_tensor(f"bank{i}", [128, 512], F32)) for i in range(8)]

        def V(name, j=None, n=1):
            o, ln = voff[name]
            if j is None:
                return vecs[:, o:o + ln]
            return vecs[:, o + j:o + j + n]

        MOD = lambda m: dv[:, m:m + 1]
        G1P = lambda k: dv[:, 96 + k:97 + k]
        G2P = lambda k: dv[:, 112 + k:113 + k]

        state = {"bank": 0, "tmp": 0, "slot": 0}

        def dbg(name, ap, reads):
            if not cfg.debug:
                return
            d = nc.dram_tensor("dbg_" + name, list(ap.shape), ap.dtype, kind="ExternalOutput").ap()
            state["ndbg"] = state.get("ndbg", 0) + 1
            P.op("sp", lambda e: e.dma_start(out=d, in_=ap), reads=reads, dma="dbg%d" % state["ndbg"])

        def bank():
            b = state["bank"]
            if b == state.get("skip"):
                b = (b + 1) % 8
            state["bank"] = (b + 1) % 8
            return banks[b], ("ps", b)

        def tmp():
            i = state["tmp"]
            state["tmp"] = (i + 1) % cfg.ntmp
            return tmps[i], ("tmp", i)

        def load_block(src_ap, nelem=SLOT_E):
            s = state["slot"]
            state["slot"] = (s + 1) % cfg.nslot
            dst = slots[s][:, 0:nelem]
            P.op("pool", lambda e, dst=dst, src=src_ap: e.dma_start(out=dst, in_=src),
                 writes=[("slot", s)], dma=f"slot{s}")
            return slots[s], ("slot", s)

        def mm_group(psap, pairs, reads, bankres, n_extra_writes=()):
            def fn(e, psap=psap, pairs=pairs):
                last = None
                n = len(pairs)
                for i, (l, r) in enumerate(pairs):
                    last = e.matmul(psap, lhsT=l, rhs=r, start=(i == 0), stop=(i == n - 1))
                return last
            P.op("pe", fn, reads=reads, writes=[bankres] + list(n_extra_writes))

        P.op("sp", lambda e: e.dma_start(out=vecs[:], in_=vecs_d), writes=["vecs"], dma="cst0")
        P.op("sp", lambda e: e.dma_start(out=bvbc[:], in_=bvbc_d), writes=["bvbc"], dma="cst1")
        P.op("sp", lambda e: e.dma_start(out=maskT[:], in_=mask_d), writes=["mask"], dma="cst2")
        P.op("sp", lambda e: e.dma_start(out=vall[:, 0, :], in_=wsp_d), writes=r_vall(0), dma="cst3")
        P.op("sp", lambda e: e.dma_start(out=vall[:, 1, :], in_=bsp_d), writes=r_vall(1), dma="cst4")
        P.op("dve", lambda e: e.memset(epsT[:], EPS), writes=["epsT"])
        P.op("sp", lambda e: e.dma_start(out=tmps[0][:, 0:128], in_=ident_d), writes=[("tmp", 0)], dma="cst5")
        P.op("dve", lambda e: e.tensor_copy(out=identb[:], in_=tmps[0][:, 0:128]), reads=[("tmp", 0)], writes=["identb"])
        P.op("dve", lambda e: e.memset(onesD[:], 1.0 / D), writes=["onesD"])
        P.op("dve", lambda e: e.memset(onesC[:], 1.0 / DC), writes=["onesC"])
        P.op("dve", lambda e: e.memset(ones1[:], 1.0), writes=["ones1"])
        P.op("dve", lambda e: e.memset(acar[:], 0.0), writes=["acar"])
        P.op("dve", lambda e: e.memset(upc[0][:], 0.0), writes=[("upc", 0)])
        P.op("act", lambda e: e.activation(out=csb[:], in_=V("cT"), func=AF.Silu), reads=["vecs"], writes=["csb"])
        for g in range(CC):
            P.op("dve", lambda e, g=g: e.tensor_tensor(out=wsb[:, g, :], in0=vall[:, 0, g * 128:(g + 1) * 128],
                                                        in1=maskT[:], op=ALU.mult),
                 reads=r_vall(0) + ["mask"], writes=[("wsb", g)])
        for half in range(2):
            bk, bres = bank()
            for gg in range(4):
                g = half * 4 + gg
                mm_group(bk[:, gg * 128:(gg + 1) * 128], [(ones1[:], wsb[:, g, :])],
                         reads=["ones1", ("wsb", g)], bankres=bres)
            for gg in range(4):
                g = half * 4 + gg
                P.op("dve", lambda e, g=g, gg=gg, bk=bk: e.scalar_tensor_tensor(
                    out=bfull[:, g, :], in0=bk[:, gg * 128:(gg + 1) * 128], scalar=V("slb", g),
                    in1=vall[:, 1, g * 128:(g + 1) * 128], op0=ALU.mult, op1=ALU.add),
                    reads=["vecs"] + r_vall(1), writes=[bres, ("bfull", g)])
        bk_mod, bres_mod = bank()
        for b in range(48):
            sl, sres = load_block(w_ada_d[b])
            wv = sl[:, :].rearrange("p (k c) -> p k c", c=256)
            for mm in range(2):
                m = 2 * b + mm
                mm_group(bk_mod[:, m:m + 1],
                         [(wv[:, k, mm * 128:(mm + 1) * 128], csb[:, k:k + 1]) for k in range(KC)],
                         reads=[sres, "csb"], bankres=bres_mod)
        P.op("dve", lambda e: e.tensor_tensor(out=dv[:, 0:96], in0=bk_mod[:, 0:96], in1=V("bada"), op=ALU.add),
             reads=["vecs"], writes=[bres_mod, "mod"])
        P.op("dve", lambda e: e.scalar_tensor_tensor(out=dv[:, 96:112], in0=dv[:, 16:32], scalar=1.0,
                                                      in1=V("n1g"), op0=ALU.add, op1=ALU.mult),
             reads=["mod", "vecs"], writes=["g1p"])
        P.op("dve", lambda e: e.scalar_tensor_tensor(out=dv[:, 112:128], in0=dv[:, 64:80], scalar=1.0,
                                                      in1=V("n2g"), op0=ALU.add, op1=ALU.mult),
             reads=["mod", "vecs"], writes=["g2p"])
        VEC = ["vecs", "mod", "g1p", "g2p"]

        def rms_to_h(T, gfn, shift_m0, dname=None):
            for k in range(KC):
                P.op("act", lambda e, k=k: e.activation(out=sqm[:, k, :T], in_=xT[:, k, :T], func=AF.Square),
                     reads=[("x", k)], writes=[("sqm", k)])
            bk, bres = bank()
            mm_group(bk[:, :T], [(onesD[:], sqm[:, k, :T]) for k in range(KC)],
                     reads=["onesD"] + [("sqm", k) for k in range(KC)], bankres=bres)
            rstd, rres_ = longs[0], ("L", 0)
            P.op("act", lambda e: e.activation(out=rstd[:, :T], in_=bk[:, :T], func=AF.Sqrt, bias=epsT[:], scale=1.0),
                 reads=["epsT"], writes=[bres, rres_])
            P.op("dve", lambda e: e.reciprocal(out=rstd[:, :T], in_=rstd[:, :T]), reads=[rres_], writes=[rres_])
            if dname is not None:
                dbg(dname + "_sq", sqm[:, :, :T], [("sqm", k) for k in range(KC)])
                dbg(dname + "_rstd", rstd[:, :T], [rres_])
                dbg(dname + "_x", xT[:, :, :T], [("x", k) for k in range(KC)])
            for k in range(KC):
                if gfn is None:
                    continue
                t_, tres = tmp()
                P.op("dve", lambda e, k=k, t_=t_: e.scalar_tensor_tensor(
                    out=t_[:, :T], in0=xT[:, k, :T], scalar=gfn(k), in1=rstd[:, :T], op0=ALU.mult, op1=ALU.mult),
                    reads=[("x", k), rres_] + VEC, writes=[tres])
                P.op("act", lambda e, k=k, t_=t_: e.activation(out=hT[:, k, :T], in_=t_[:, :T], func=AF.Identity,
                                                               bias=MOD(shift_m0 + k), scale=1.0),
                     reads=[tres] + VEC, writes=[("h", k)])
            return rstd, rres_

        HALL = [("h", k) for k in range(KC)]

        def wv256(sl):
            return sl[:, :].rearrange("p (k c) -> p k c", c=256)

        def wv512(sl):
            return sl[:, :].rearrange("p (k c) -> p k c", c=512)

        def do_tile(ti, T, t0):
            halo = (ti == 0)
            nT = T // 128
            cur, nxt = ti % 2, (ti + 1) % 2
            P.op("sp", lambda e, t0=t0, T=T: e.dma_start(out=xT[:, :, :T], in_=xd[:, :, t0:t0 + T]),
                 writes=[("x", k) for k in range(KC)], dma="xin")
            rms_to_h(T, G1P, 0, "n1" if ti == 1 else None)

            if ti == 1:
                dbg("mod", dv[:, :], VEC)
                dbg("h", hT[:, :, :T], HALL)
            P.op("dve", lambda e: e.tensor_copy(out=abuf[:, :, 0:HALO], in_=acar[:]),
                 reads=["acar"], writes=sum([r_a(j) for j in range(CC)], []))
            for jp in range(4):
                slv, rv = load_block(w_in_d[jp])
                slg, rg = load_block(w_in_d[4 + jp])
                for jj in range(2):
                    j = 2 * jp + jj
                    bkv, brv = bank()
                    bkg, brg = bank()
                    mm_group(bkv[:, :T], [(wv256(slv)[:, k, jj * 128:(jj + 1) * 128], hT[:, k, :T]) for k in range(KC)],
                             reads=[rv] + HALL, bankres=brv)
                    mm_group(bkg[:, :T], [(wv256(slg)[:, k, jj * 128:(jj + 1) * 128], hT[:, k, :T]) for k in range(KC)],
                             reads=[rg] + HALL, bankres=brg)
                    sg, sgr = tmp()
                    P.op("act", lambda e, j=j, bkg=bkg, sg=sg: e.activation(
                        out=sg[:, :T], in_=bkg[:, :T], func=AF.Sigmoid, bias=V("b_agate", j), scale=1.0),
                        reads=VEC, writes=[brg, sgr])
                    P.op("dve", lambda e, j=j, bkv=bkv, sg=sg: e.scalar_tensor_tensor(
                        out=abuf[:, j, HALO:HALO + T], in0=bkv[:, :T], scalar=V("b_aval", j), in1=sg[:, :T],
                        op0=ALU.add, op1=ALU.mult), reads=VEC + [sgr], writes=[brv] + r_a(j))

            conv_fill = []

            def conv_chunk(j, T=T, halo=halo):
                o_cdw = voff["cdw"][0]
                wj = vecs[:, o_cdw + j * CONV_K:o_cdw + (j + 1) * CONV_K]
                P.op("dve", lambda e: e.tensor_tensor(
                    out=dg[:, :, :], in0=identb[:, :].unsqueeze(1).broadcast_to([128, CONV_K, 128]),
                    in1=wj.unsqueeze(2).broadcast_to([128, CONV_K, 128]), op=ALU.mult),
                    reads=VEC + ["identb"], writes=["dg"])
                bkc, brc = bank()
                mm_group(bkc[:, :T], [(dg[:, k, :], abuf[:, j, k:k + T]) for k in range(CONV_K)],
                         reads=["dg"] + r_a(j), bankres=brc)
                P.op("act", lambda e: e.activation(out=acb[:, j, :T], in_=bkc[:, :T], func=AF.Identity,
                                                   bias=V("cdb", j), scale=1.0),
                     reads=VEC, writes=[brc] + r_ac(j))
                if halo:
                    P.op("dve", lambda e: e.tensor_scalar(out=acar[:, j, :], in0=abuf[:, j, T:T + HALO],
                                                          scalar1=V("flag", 0), scalar2=None, op0=ALU.mult),
                         reads=VEC + r_a(j), writes=["acar"])
                else:
                    P.op("dve", lambda e: e.tensor_copy(out=acar[:, j, :], in_=abuf[:, j, T:T + HALO]),
                         reads=r_a(j), writes=["acar"])

            for j in range(CC):
                conv_fill.append(lambda j=j: conv_chunk(j))

            def fill(n=1):
                for _ in range(n):
                    if conv_fill:
                        conv_fill.pop(0)()

            vblk = [load_block(w_in_d[12 + i]) for i in range(4)]
            for n in range(nT):
                bkA, brA = bank()
                bkB, brB = bank()
                for (bk_, br_, b0) in ((bkA, brA, 0), (bkB, brB, 2)):
                    for bb in range(2):
                        sl, sres = vblk[b0 + bb]
                        mm_group(bk_[:, bb * 256:(bb + 1) * 256],
                                 [(hT[:, k, n * 128:(n + 1) * 128], wv256(sl)[:, k, :]) for k in range(KC)],
                                 reads=[sres] + HALL, bankres=br_)
                P.op("dve", lambda e, n=n, bkA=bkA: e.tensor_tensor(out=vall[:, n, 0:512], in0=bkA[:, :],
                                                                    in1=bvbc[:, 0:512], op=ALU.add),
                     reads=["bvbc"], writes=[brA] + r_vall(n))
                P.op("dve", lambda e, n=n, bkB=bkB: e.tensor_tensor(out=vall[:, n, 512:1024], in0=bkB[:, :],
                                                                    in1=bvbc[:, 512:1024], op=ALU.add),
                     reads=["bvbc"], writes=[brB] + r_vall(n))
                P.op("act", lambda e, n=n: e.activation(out=vall[:, n, :], in_=vall[:, n, :], func=AF.Gelu),
                     reads=r_vall(n), writes=r_vall(n))
                P.op("dve", lambda e, n=n: e.bn_stats(out=bnst[:, n, 0:6], in_=vall[:, n, 0:512]),
                     reads=r_vall(n), writes=["bnst"])
                P.op("dve", lambda e, n=n: e.bn_stats(out=bnst[:, n, 6:12], in_=vall[:, n, 512:1024]),
                     reads=r_vall(n), writes=["bnst"])
                P.op("dve", lambda e, n=n: e.bn_aggr(out=bnmv[:, n, 0:2], in_=bnst[:, n, :]),
                     reads=["bnst"], writes=["bnmv"])
            P.op("act", lambda e, nT=nT: e.activation(out=bnmv[:, 0:nT, 2], in_=bnmv[:, 0:nT, 1], func=AF.Sqrt,
                                                      bias=epsT[:], scale=1.0),
                 reads=["bnmv", "epsT"], writes=["bnmv"])
            P.op("dve", lambda e, nT=nT: e.reciprocal(out=bnmv[:, 0:nT, 2], in_=bnmv[:, 0:nT, 2]),
                 reads=["bnmv"], writes=["bnmv"])
            for n in range(nT):
                P.op("dve", lambda e, n=n: e.tensor_scalar(out=vn[:, n, :], in0=vall[:, n, :],
                                                           scalar1=bnmv[:, n, 0:1], scalar2=bnmv[:, n, 2:3],
                                                           op0=ALU.subtract, op1=ALU.mult),
                     reads=r_vall(n) + ["bnmv"], writes=r_vn(n))

            for gp in range(4):
                slu, ru = load_block(w_in_d[8 + gp])
                for gg in range(2):
                    g = 2 * gp + gg
                    bku, bru = bank()
                    bks, brs = bank()
                    mm_group(bku[:, :T], [(wv256(slu)[:, k, gg * 128:(gg + 1) * 128], hT[:, k, :T]) for k in range(KC)],
                             reads=[ru] + HALL, bankres=bru)
                    for n in range(nT):
                        mm_group(bks[:, n * 128:(n + 1) * 128], [(vn[:, n, g * 128:(g + 1) * 128], wsb[:, g, :])],
                                 reads=[("wsb", g)] + r_vn(n), bankres=brs)
                    gu, gur = tmp()
                    P.op("act", lambda e, g=g, bku=bku, gu=gu: e.activation(
                        out=gu[:, :T], in_=bku[:, :T], func=AF.Gelu, bias=V("b_u", g), scale=1.0),
                        reads=VEC, writes=[bru, gur])
                    t2, t2r = tmp()
                    for n in range(nT):
                        P.op("dve", lambda e, g=g, n=n, bks=bks, t2=t2: e.scalar_tensor_tensor(
                            out=t2[:, n * 128:(n + 1) * 128], in0=bks[:, n * 128:(n + 1) * 128], scalar=V("slg", g),
                            in1=bfull[:, g, :], op0=ALU.mult, op1=ALU.add),
                            reads=VEC + [("bfull", g)], writes=[brs, t2r])
                    P.op("dve", lambda e, g=g, t2=t2, gu=gu: e.tensor_tensor(out=ub[:, g, :T], in0=t2[:, :T],
                                                                            in1=gu[:, :T], op=ALU.mult),
                         reads=[t2r, gur], writes=[("ub", g)])
                    fill(1)
            fill(CC)

            if ti == 1:
                dbg("abuf", abuf[:, :, 0:HALO + T], sum([r_a(j) for j in range(CC)], []))
                dbg("vn", vn[:, 0:nT, :], sum([r_vn(n) for n in range(nT)], []))
                dbg("ub", ub[:, :, :T], [("ub", k) for k in range(CC)])
                dbg("ac", acb[:, :, :T], sum([r_ac(j) for j in range(CC)], []))
            for j in range(CC):
                P.op("act", lambda e, j=j: e.activation(out=sqm[:, j, :T], in_=acb[:, j, :T], func=AF.Identity),
                     reads=r_ac(j), writes=[("sqm", j)])
                P.op("act", lambda e, j=j: e.activation(out=sqm[:, CC + j, :T], in_=acb[:, j, :T], func=AF.Square),
                     reads=r_ac(j), writes=[("sqm", CC + j)])
            bkm, brm = bank()
            bkq, brq = bank()
            mm_group(bkm[:, :T], [(onesC[:], sqm[:, j, :T]) for j in range(CC)],
                     reads=["onesC"] + [("sqm", j) for j in range(CC)], bankres=brm)
            mm_group(bkq[:, :T], [(onesC[:], sqm[:, CC + j, :T]) for j in range(CC)],
                     reads=["onesC"] + [("sqm", CC + j) for j in range(CC)], bankres=brq)
            mean, meanr = longs[1], ("L", 1)
            P.op("act", lambda e: e.activation(out=mean[:, :T], in_=bkm[:, :T], func=AF.Identity),
                 writes=[brm, meanr])
            m2, m2r = longs[2], ("L", 2)
            P.op("dve", lambda e: e.tensor_tensor(out=m2[:, :T], in0=mean[:, :T], in1=mean[:, :T], op=ALU.mult),
                 reads=[meanr], writes=[m2r])
            P.op("dve", lambda e: e.tensor_tensor(out=m2[:, :T], in0=bkq[:, :T], in1=m2[:, :T], op=ALU.subtract),
                 reads=[m2r], writes=[brq, m2r])
            P.op("act", lambda e: e.activation(out=m2[:, :T], in_=m2[:, :T], func=AF.Sqrt, bias=epsT[:], scale=1.0),
                 reads=[m2r, "epsT"], writes=[m2r])
            P.op("dve", lambda e: e.reciprocal(out=m2[:, :T], in_=m2[:, :T]), reads=[m2r], writes=[m2r])
            for j in range(CC):
                lt, ltr = tmp()
                P.op("dve", lambda e, j=j, lt=lt: e.tensor_tensor(out=lt[:, :T], in0=acb[:, j, :T], in1=mean[:, :T],
                                                                  op=ALU.subtract),
                     reads=r_ac(j) + [meanr], writes=[ltr])
                P.op("dve", lambda e, lt=lt: e.tensor_tensor(out=lt[:, :T], in0=lt[:, :T], in1=m2[:, :T], op=ALU.mult),
                     reads=[ltr, m2r], writes=[ltr])
                P.op("act", lambda e, j=j, lt=lt: e.activation(out=a2[:, j, :T], in_=lt[:, :T], func=AF.Silu,
                                                               bias=V("clb", j), scale=V("clg", j)),
                     reads=[ltr] + VEC, writes=r_a2(j))

            cur_blocks = {}
            for j in range(KC):
                if j % 2 == 0:
                    cur_blocks["ga"] = load_block(w_in_d[16 + j // 2])
                    cur_blocks["gb"] = load_block(w_in_d[24 + j // 2])
                if j % 4 == 0:
                    cur_blocks["so"] = load_block(w_so_d[j // 4])
                    cur_blocks["co"] = load_block(w_co_d[j // 4])
                (sga_, rga), (sgb_, rgb) = cur_blocks["ga"], cur_blocks["gb"]
                (sso, rso), (sco, rco) = cur_blocks["so"], cur_blocks["co"]
                jj = j % 2
                j4 = j % 4
                bkga, brga = bank()
                bkgb, brgb = bank()
                bkyb, bryb = bank()
                bkya, brya = bank()
                mm_group(bkga[:, :T], [(wv256(sga_)[:, k, jj * 128:(jj + 1) * 128], hT[:, k, :T]) for k in range(KC)],
                         reads=[rga] + HALL, bankres=brga)
                mm_group(bkgb[:, :T], [(wv256(sgb_)[:, k, jj * 128:(jj + 1) * 128], hT[:, k, :T]) for k in range(KC)],
                         reads=[rgb] + HALL, bankres=brgb)
                mm_group(bkyb[:, :T], [(wv512(sso)[:, k, j4 * 128:(j4 + 1) * 128], ub[:, k, :T]) for k in range(CC)],
                         reads=[rso] + [("ub", k) for k in range(CC)], bankres=bryb)
                mm_group(bkya[:, :T], [(wv512(sco)[:, k, j4 * 128:(j4 + 1) * 128], a2[:, k, :T]) for k in range(CC)],
                         reads=[rco] + sum([r_a2(k) for k in range(CC)], []), bankres=brya)
                sa, sar = tmp()
                sb_, sbr = tmp()
                P.op("act", lambda e, j=j, bkga=bkga, sa=sa: e.activation(
                    out=sa[:, :T], in_=bkga[:, :T], func=AF.Sigmoid, bias=V("b_ga", j), scale=1.0),
                    reads=VEC, writes=[brga, sar])
                P.op("act", lambda e, j=j, bkgb=bkgb, sb_=sb_: e.activation(
                    out=sb_[:, :T], in_=bkgb[:, :T], func=AF.Sigmoid, bias=V("b_gb", j), scale=1.0),
                    reads=VEC, writes=[brgb, sbr])
                P.op("dve", lambda e, bkyb=bkyb, sb_=sb_: e.tensor_tensor(out=sb_[:, :T], in0=sb_[:, :T], in1=bkyb[:, :T],
                                                                          op=ALU.mult),
                     reads=[sbr], writes=[bryb, sbr])
                P.op("dve", lambda e, bkya=bkya, sa=sa: e.tensor_tensor(out=sa[:, :T], in0=sa[:, :T], in1=bkya[:, :T],
                                                                        op=ALU.mult),
                     reads=[sar], writes=[brya, sar])
                P.op("dve", lambda e, j=j, sa=sa, sb_=sb_: e.tensor_tensor(out=sqm[:, j, :T], in0=sa[:, :T], in1=sb_[:, :T],
                                                                           op=ALU.add),
                     reads=[sar, sbr], writes=[("sqm", j)])

            if ti == 1:
                dbg("a2", a2[:, :, :T], sum([r_a2(j) for j in range(CC)], []))
                dbg("merged", sqm[:, :, :T], [("sqm", k) for k in range(KC)])
            for j in range(KC):
                if j % 2 == 0:
                    swo, rwo = load_block(w_out_d[j // 2])
                jj = j % 2
                bk, br = bank()
                mm_group(bk[:, :T], [(wv256(swo)[:, k, jj * 128:(jj + 1) * 128], sqm[:, k, :T]) for k in range(KC)],
                         reads=[rwo] + [("sqm", k) for k in range(KC)], bankres=br)
                P.op("dve", lambda e, j=j, bk=bk: e.scalar_tensor_tensor(
                    out=xT[:, j, :T], in0=bk[:, :T], scalar=MOD(32 + j), in1=xT[:, j, :T], op0=ALU.mult, op1=ALU.add),
                    reads=VEC + [("x", j)], writes=[br, ("x", j)])

            if ti == 1:
                dbg("x1", xT[:, :, :T], [("x", k) for k in range(KC)])
            rms_to_h(T, G2P, 48)

            def fd(jj, k):
                return V("fdw", jj * 3 + k)

            o_fdw = voff["fdw"][0]
            fdw3 = vecs[:, o_fdw:o_fdw + 2 * FC * 3].rearrange("p (j k) -> p j k", k=3)
            first_own = (ti == 1)
            if halo:
                P.op("dve", lambda e: e.tensor_scalar(out=hhalo[:, :, :], in0=hT[:, :, T - 2:T], scalar1=V("flag", 0),
                                                      scalar2=None, op0=ALU.mult), reads=VEC + HALL, writes=["hh"])
                return
            if first_own:
                bk_small, br_small = bank()
                state["skip"] = int(br_small[1])
            if not first_own:
                P.op("dve", lambda e, cur=cur: e.tensor_tensor(out=cw[:, :, 0], in0=fdw3[:, :, 0], in1=upc[cur][:, :, 0],
                                                               op=ALU.mult), reads=VEC + [("upc", cur)], writes=["cw"])
                P.op("dve", lambda e, cur=cur: e.tensor_tensor(out=cw[:, :, 1], in0=fdw3[:, :, 1], in1=upc[cur][:, :, 1],
                                                               op=ALU.mult), reads=VEC + [("upc", cur)], writes=["cw"])
                P.op("dve", lambda e: e.tensor_tensor(out=cw[:, :, 0], in0=cw[:, :, 0], in1=cw[:, :, 1], op=ALU.add),
                     reads=["cw"], writes=["cw"])
                P.op("dve", lambda e, cur=cur: e.tensor_tensor(out=cw[:, :, 1], in0=fdw3[:, :, 0], in1=upc[cur][:, :, 1],
                                                               op=ALU.mult), reads=VEC + [("upc", cur)], writes=["cw"])
            for jf in range(FC):
                sup, rup = load_block(w_up_d[jf])
                outs = []
                for half, jj in ((0, jf), (1, FC + jf)):
                    if first_own:
                        mm_group(bk_small[:, 2 * jj:2 * jj + 2],
                                 [(wv256(sup)[:, k, half * 128:(half + 1) * 128], hhalo[:, k, :]) for k in range(KC)],
                                 reads=[rup, "hh"], bankres=br_small)
                        P.op("dve", lambda e, jj=jj: e.tensor_scalar(
                            out=cw[:, jj, 0:1], in0=bk_small[:, 2 * jj:2 * jj + 1], scalar1=fd(jj, 0), scalar2=None,
                            op0=ALU.mult), reads=VEC, writes=[br_small, "cw"])
                        P.op("dve", lambda e, jj=jj: e.scalar_tensor_tensor(
                            out=cw[:, jj, 0:1], in0=bk_small[:, 2 * jj + 1:2 * jj + 2], scalar=fd(jj, 1),
                            in1=cw[:, jj, 0:1], op0=ALU.mult, op1=ALU.add), reads=VEC + ["cw"], writes=[br_small, "cw"])
                        P.op("dve", lambda e, jj=jj: e.tensor_scalar(
                            out=cw[:, jj, 1:2], in0=bk_small[:, 2 * jj + 1:2 * jj + 2], scalar1=fd(jj, 0), scalar2=None,
                            op0=ALU.mult), reads=VEC, writes=[br_small, "cw"])
                    bk, br = bank()
                    mm_group(bk[:, :T], [(wv256(sup)[:, k, half * 128:(half + 1) * 128], hT[:, k, :T]) for k in range(KC)],
                             reads=[rup] + HALL, bankres=br)
                    if halo:
                        P.op("dve", lambda e, jj=jj, bk=bk, nxt=nxt: e.tensor_scalar(
                            out=upc[nxt][:, jj, :], in0=bk[:, T - 2:T], scalar1=V("flag", 0), scalar2=None, op0=ALU.mult),
                            reads=VEC, writes=[br, ("upc", nxt)])
                        continue
                    o_, orr = tmp()
                    P.op("act", lambda e, jj=jj, bk=bk, o_=o_: e.activation(
                        out=o_[:, :T], in_=bk[:, :T], func=AF.Identity, bias=V("fdb", jj), scale=fd(jj, 2)),
                        reads=VEC, writes=[br, orr])
                    P.op("dve", lambda e, jj=jj, bk=bk, o_=o_: e.scalar_tensor_tensor(
                        out=o_[:, 1:T], in0=bk[:, 0:T - 1], scalar=fd(jj, 1), in1=o_[:, 1:T], op0=ALU.mult, op1=ALU.add),
                        reads=VEC + [orr], writes=[br, orr])
                    P.op("dve", lambda e, jj=jj, bk=bk, o_=o_: e.scalar_tensor_tensor(
                        out=o_[:, 2:T], in0=bk[:, 0:T - 2], scalar=fd(jj, 0), in1=o_[:, 2:T], op0=ALU.mult, op1=ALU.add),
                        reads=VEC + [orr], writes=[br, orr])
                    P.op("dve", lambda e, jj=jj, o_=o_: e.tensor_tensor(out=o_[:, 0:2], in0=o_[:, 0:2], in1=cw[:, jj, :],
                                                                        op=ALU.add),
                         reads=["cw", orr], writes=[orr])
                    P.op("dve", lambda e, jj=jj, bk=bk, nxt=nxt: e.tensor_copy(out=upc[nxt][:, jj, :], in_=bk[:, T - 2:T]),
                         writes=[br, ("upc", nxt)])
                    outs.append((o_, orr))
                if halo:
                    continue
                (ov, ovr), (og, ogr) = outs
                P.op("act", lambda e, og=og: e.activation(out=og[:, :T], in_=og[:, :T], func=AF.Silu),
                     reads=[ogr], writes=[ogr])
                P.op("dve", lambda e, jf=jf, ov=ov, og=og: e.tensor_tensor(out=gbuf[:, jf, :T], in0=og[:, :T], in1=ov[:, :T],
                                                                           op=ALU.mult),
                     reads=[ovr, ogr], writes=r_g(jf))

            state["skip"] = None
            if ti == 1:
                dbg("g", gbuf[:, :, :T], sum([r_g(k) for k in range(FC)], []))
                dbg("cw", cw[:, :, :], ["cw"])
            if not halo:
                for j in range(KC):
                    sd0, rd0 = load_block(w_dn_d[2 * j], HK * 128)
                    sd1, rd1 = load_block(w_dn_d[2 * j + 1], HK * 128)
                    bk, br = bank()
                    pairs = []
                    for hh, sd in ((0, sd0), (1, sd1)):
                        wv = sd[:, 0:HK * 128].rearrange("p (k c) -> p k c", c=128)
                        for kk in range(HK):
                            pairs.append((wv[:, kk, :], gbuf[:, hh * HK + kk, :T]))
                    mm_group(bk[:, :T], pairs, reads=[rd0, rd1] + sum([r_g(k) for k in range(FC)], []), bankres=br)
                    P.op("dve", lambda e, j=j, bk=bk: e.scalar_tensor_tensor(
                        out=xT[:, j, :T], in0=bk[:, :T], scalar=MOD(80 + j), in1=xT[:, j, :T], op0=ALU.mult, op1=ALU.add),
                        reads=VEC + [("x", j)], writes=[br, ("x", j)])
                if ti == 1:
                    dbg("x2", xT[:, :, :T], [("x", k) for k in range(KC)])
                rstd, rres_ = rms_to_h(T, None, 0)
                for k in range(KC):
                    P.op("dve", lambda e, k=k, rstd=rstd: e.scalar_tensor_tensor(
                        out=xT[:, k, :T], in0=xT[:, k, :T], scalar=V("fg", k), in1=rstd[:, :T], op0=ALU.mult, op1=ALU.mult),
                        reads=VEC + [("x", k), rres_], writes=[("x", k)])
                P.op("sp", lambda e, t0=t0, T=T: e.dma_start(out=yd[:, :, t0 - 128:t0 - 128 + T], in_=xT[:, :, :T]),
                     reads=[("x", k) for k in range(KC)], dma="yout")

        t0_ = 0
        for ti_, T_ in enumerate(cfg.tiles):
            do_tile(ti_, T_, t0_)
            t0_ += T_

        n_out = P.dcnt.get("yout", 0)

        names = P.sem_names()
        sems = {n: st.enter_context(nc.semaphore(n)) for n in names}
        block = st.enter_context(nc.Block())

        @block.tensor
        def _(e):
            P.emit("pe", e, sems)

        @block.scalar
        def _(e):
            P.emit("act", e, sems)

        @block.vector
        def _(e):
            P.emit("dve", e, sems)

        @block.gpsimd
        def _(e):
            P.emit("pool", e, sems)

        @block.sync
        def _(e):
            P.emit("sp", e, sems)
            e.wait_ge(sems["d_yout"], 16 * n_out)
            for i in range(1, state.get("ndbg", 0) + 1):
                e.wait_ge(sems["d_dbg%d" % i], 16)
    return nc


def _blk_k(W, kc, cols):
    K, N = W.shape
    nb = N // cols
    return np.ascontiguousarray(W.reshape(kc, 128, nb, cols).transpose(2, 1, 0, 3).reshape(nb, 128, kc * cols))


def _fm(v, n):
    return np.asarray(v, np.float32).reshape(n, 128).T


def prep_shared(cfg, inp):
    FC, HK, DFF = cfg.FC, cfg.HK, cfg.DFF
    f = lambda a: np.asarray(a, np.float32)
    sh = {}
    sh["w_ada"] = _blk_k(f(inp["w_ada"])[0], KC, 256)
    sh["w_in"] = _blk_k(f(inp["w_in"])[0], KC, 256)
    sh["w_co"] = _blk_k(f(inp["w_conv_out"])[0], CC, 512)
    sh["w_so"] = _blk_k(f(inp["w_sgu_out"])[0], CC, 512)
    sh["w_out"] = _blk_k(f(inp["w_out"])[0], KC, 256)
    wu = f(inp["w_up"])[0]
    wu2 = np.concatenate([wu[:, :DFF].reshape(D, FC, 1, 128), wu[:, DFF:].reshape(D, FC, 1, 128)], axis=2)
    sh["w_up"] = _blk_k(wu2.reshape(D, FC * 256), KC, 256)
    wd = f(inp["w_down"])[0]
    sh["w_dn"] = np.ascontiguousarray(
        wd.reshape(2, HK, 128, KC, 128).transpose(3, 0, 2, 1, 4).reshape(2 * KC, 128, HK * 128))
    b_in = f(inp["b_in"])[0]
    sh["bvbc"] = np.ascontiguousarray(np.broadcast_to(b_in[3072:4096][None, :], (128, 1024)))
    sh["bsp"] = np.ascontiguousarray(np.broadcast_to(f(inp["b_spatial"])[0].reshape(1, 1024), (128, 1024)))
    sh["wspT"] = np.ascontiguousarray(f(inp["w_spatial"])[0].transpose(2, 0, 1).reshape(128, 1024))
    sh["maskT"] = np.ascontiguousarray(np.triu(np.ones((128, 128), np.float32)))
    sh["ident"] = np.ascontiguousarray(np.eye(128, dtype=np.float32))
    voff, NV = vec_layout(cfg)
    vecs = np.zeros((128, NV), np.float32)

    def put(name, arr):
        o, n = voff[name]
        assert arr.shape == (128, n), (name, arr.shape, n)
        vecs[:, o:o + n] = arr

    put("n1g", _fm(f(inp["norm1_g"])[0], KC))
    put("n2g", _fm(f(inp["norm2_g"])[0], KC))
    put("fg", _fm(f(inp["final_g"]), KC))
    put("b_aval", _fm(b_in[0:1024], CC))
    put("b_agate", _fm(b_in[1024:2048], CC))
    put("b_u", _fm(b_in[2048:3072], CC))
    put("b_ga", _fm(b_in[4096:6144], KC))
    put("b_gb", _fm(b_in[6144:8192], KC))
    put("cdw", np.ascontiguousarray(f(inp["conv_dw_w"])[0].reshape(CONV_K, CC, 128).transpose(2, 1, 0)).reshape(128, CC * CONV_K))
    put("cdb", _fm(f(inp["conv_dw_b"])[0], CC))
    put("clg", _fm(f(inp["conv_ln_g"])[0], CC))
    put("clb", _fm(f(inp["conv_ln_b"])[0], CC))
    put("slg", _fm(f(inp["sgu_ln_g"])[0], CC))
    put("slb", _fm(f(inp["sgu_ln_b"])[0], CC))
    put("fdw", np.ascontiguousarray(f(inp["ffn_dw_w"])[0].reshape(3, 2 * FC, 128).transpose(2, 1, 0)).reshape(128, 2 * FC * 3))
    put("fdb", _fm(f(inp["ffn_dw_b"])[0], 2 * FC))
    put("bada", _fm(f(inp["b_ada"])[0], 96))
    sh["vecs"] = vecs
    return sh, voff


def make_in_maps(cfg, inp):
    x = np.asarray(inp["x"], np.float32)
    c = np.asarray(inp["c"], np.float32)
    B, S, _ = x.shape
    per_seq = NCORES // B
    own = cfg.NOUT
    assert per_seq * own == S
    sh, voff = prep_shared(cfg, inp)
    in_maps = []
    for core in range(NCORES):
        b, q = divmod(core, per_seq)
        s0 = q * own
        xc = np.zeros((128 + own, D), np.float32)
        xc[128:] = x[b, s0:s0 + own]
        if q > 0:
            xc[:128] = x[b, s0 - 128:s0]
        xT = np.ascontiguousarray(xc.reshape(128 + own, KC, 128).transpose(2, 1, 0))
        vecs = sh["vecs"].copy()
        o, n = voff["flag"]
        vecs[:, o] = 1.0 if q > 0 else 0.0
        o, n = voff["cT"]
        vecs[:, o:o + n] = _fm(c[b], KC)
        m = {k: v for k, v in sh.items() if k != "vecs"}
        m["vecs"] = vecs
        m["xT"] = xT
        in_maps.append(m)
    return in_maps


def kernel(**inp):
    cfg = CFG
    x = np.asarray(inp["x"], np.float32)
    B, S, _ = x.shape
    per_seq = NCORES // B
    own = cfg.NOUT
    in_maps = make_in_maps(cfg, inp)
    nc = build_program(cfg)
    res = run_bass_kernel_spmd(nc, in_maps, core_ids=list(range(NCORES)))
    global LAST_RES
    LAST_RES = res.results
    out = np.empty((B, S, D), np.float32)
    for core in range(NCORES):
        b, q = divmod(core, per_seq)
        yT = np.asarray(res.results[core]["yT"])
        out[b, q * own:(q + 1) * own] = yT.transpose(2, 1, 0).reshape(own, D)
    return out
```

```python
import numpy as np
from contextlib import ExitStack
import concourse.bass as bass
import concourse.mybir as mybir
from concourse.bass_utils import run_bass_kernel_spmd

F32 = mybir.dt.float32
BF16 = mybir.dt.bfloat16
AF = mybir.ActivationFunctionType
ALU = mybir.AluOpType

D = 2048
KC = 16
DC = 1024
CC = 8
CONV_K = 31
HALO = CONV_K - 1
EPS = 1e-6
NCORES = 8
SLOT_E = 4096


class Cfg:
    def __init__(self, dff=5632, tiles=(128, 512, 512, 512, 512), nslot=5, ntmp=5):
        self.DFF = dff
        self.FC = dff // 128
        self.HK = self.FC // 2
        self.tiles = tuple(tiles)
        self.TM = max(tiles)
        self.NTOK = sum(tiles)
        self.NOUT = self.NTOK - 128
        self.nslot = nslot
        self.ntmp = ntmp
        self.debug = False


CFG = Cfg()


def vec_layout(cfg):
    FC = cfg.FC
    items = [("n1g", KC), ("n2g", KC), ("fg", KC), ("b_aval", CC), ("b_agate", CC), ("b_u", CC),
             ("b_ga", KC), ("b_gb", KC), ("cdw", CC * CONV_K), ("cdb", CC), ("clg", CC), ("clb", CC),
             ("slg", CC), ("slb", CC), ("fdw", 2 * FC * 3), ("fdb", 2 * FC), ("bada", 96),
             ("flag", 1), ("cT", KC)]
    off = {}
    o = 0
    for name, n in items:
        off[name] = (o, n)
        o += n
    return off, o


class Prog:
    ENG = ("pe", "act", "dve", "pool", "sp")

    def __init__(self):
        self.q = {e: [] for e in self.ENG}
        self.cnt = {e: 0 for e in self.ENG}
        self.waited = {e: {} for e in self.ENG}
        self.lastw = {}
        self.readers = {}
        self.dcnt = {}

    def op(self, eng, fn, reads=(), writes=(), dma=None):
        deps = {}
        for r in reads:
            t = self.lastw.get(r)
            if t is not None:
                deps[t] = True
        for w in writes:
            t = self.lastw.get(w)
            if t is not None:
                deps.setdefault(t, False)
            for t in self.readers.get(w, ()):
                deps.setdefault(t, False)
        if dma is not None:
            n = self.dcnt.get(dma, 0) + 1
            self.dcnt[dma] = n
            tok = ("d_" + dma, 16 * n, None)
        else:
            self.cnt[eng] += 1
            tok = ("e_" + eng, self.cnt[eng], eng)
        wd = self.waited[eng]
        need = {}
        for (sem, val, te), raw in deps.items():
            if te == eng and eng == "pe":
                continue
            if wd.get(sem, 0) >= val:
                continue
            if need.get(sem, 0) < val:
                need[sem] = val
        for sem, val in need.items():
            wd[sem] = val
        self.q[eng].append((list(need.items()), fn, (tok[0], 16 if dma is not None else 1)))
        for r in reads:
            self.readers.setdefault(r, []).append(tok)
        for w in writes:
            self.lastw[w] = tok
            self.readers[w] = []
        return tok

    def sem_names(self):
        s = set()
        for e in self.ENG:
            for waits, _, inc in self.q[e]:
                s.add(inc[0])
                for sem, _ in waits:
                    s.add(sem)
        return sorted(s)

    def emit(self, eng, handle, sems):
        for waits, fn, inc in self.q[eng]:
            for sem, val in waits:
                handle.wait_ge(sems[sem], val)
            ins = fn(handle)
            ins.then_inc(sems[inc[0]], inc[1])


def build_program(cfg):
    FC, HK, TM, NTOK, NOUT = cfg.FC, cfg.HK, cfg.TM, cfg.NTOK, cfg.NOUT
    nc = bass.Bass("TRN2", target_bir_lowering=False)
    P = Prog()
    voff, NV = vec_layout(cfg)

    def din(name, shape):
        return nc.dram_tensor(name, list(shape), F32, kind="ExternalInput").ap()

    xd = din("xT", [128, KC, NTOK])
    vecs_d = din("vecs", [128, NV])
    bvbc_d = din("bvbc", [128, 1024])
    bsp_d = din("bsp", [128, 1024])
    wsp_d = din("wspT", [128, 1024])
    mask_d = din("maskT", [128, 128])
    ident_d = din("ident", [128, 128])
    w_ada_d = din("w_ada", [48, 128, SLOT_E])
    w_in_d = din("w_in", [32, 128, SLOT_E])
    w_co_d = din("w_co", [4, 128, SLOT_E])
    w_so_d = din("w_so", [4, 128, SLOT_E])
    w_out_d = din("w_out", [8, 128, SLOT_E])
    w_up_d = din("w_up", [FC, 128, SLOT_E])
    w_dn_d = din("w_dn", [32, 128, HK * 128])
    yd = nc.dram_tensor("yT", [128, KC, NOUT], F32, kind="ExternalOutput").ap()

    with ExitStack() as st:
        def sb(name, shape, dt):
            return st.enter_context(nc.sbuf_tensor(name, list(shape), dt))

        xT = sb("xTs", [128, KC, TM], F32)
        hT = sb("hT", [128, KC, TM], BF16)
        sqm = sb("sqm", [128, KC, TM], BF16)
        ub = sb("ub", [128, CC, TM], BF16)
        AW = HALO + TM
        AWW = CC * AW // 2
        o_a, o_ac, o_a2, o_vn = 0, AWW, AWW + CC * TM, AWW + CC * TM + CC * TM // 2
        RW = max(o_vn + 4 * 1024 // 2, FC * TM // 2)
        R = sb("R", [128, RW], F32)
        abuf = R[:, o_a:o_a + AWW].bitcast(BF16).rearrange("p (c t) -> p c t", t=AW)
        acb = R[:, o_ac:o_ac + CC * TM].rearrange("p (c t) -> p c t", t=TM)
        a2 = R[:, o_a2:o_a2 + CC * TM // 2].bitcast(BF16).rearrange("p (c t) -> p c t", t=TM)
        vn = R[:, o_vn:o_vn + 2048].bitcast(BF16).rearrange("p (n c) -> p n c", c=1024)
        gbuf = R[:, 0:FC * TM // 2].bitcast(BF16).rearrange("p (c t) -> p c t", t=TM)
        vall = R[:, o_ac:o_ac + 4096].rearrange("p (n c) -> p n c", c=1024)
        ystage = R[:, 0:KC * TM].rearrange("p (c t) -> p c t", t=TM)
        CELL = 256

        def rres(lo, hi):
            return [("R", c) for c in range(lo // CELL, (hi - 1) // CELL + 1)]

        def r_a(j):
            return rres(o_a + j * AW // 2, o_a + (j + 1) * AW // 2)

        def r_ac(j):
            return rres(o_ac + j * TM, o_ac + (j + 1) * TM)

        def r_a2(j):
            return rres(o_a2 + j * TM // 2, o_a2 + (j + 1) * TM // 2)

        def r_vn(n):
            return rres(o_vn + n * 512, o_vn + (n + 1) * 512)

        def r_vall(n):
            return rres(o_ac + n * 1024, o_ac + (n + 1) * 1024)

        def r_g(j):
            return rres(j * TM // 2, (j + 1) * TM // 2)

        slots = [sb(f"slot{i}", [128, SLOT_E], BF16) for i in range(cfg.nslot)]
        tmps = [sb(f"tmp{i}", [128, TM], F32) for i in range(cfg.ntmp)]
        longs = [sb(f"long{i}", [128, TM], F32) for i in range(3)]
        vecs = sb("vecs_s", [128, NV], F32)
        dv = sb("dv", [128, 96 + 2 * KC], F32)
        bvbc = sb("bvbc_s", [128, 1024], F32)
        bfull = sb("bfull", [128, CC, 128], F32)
        maskT = sb("mask_s", [128, 128], F32)
        wsb = sb("wsb", [128, CC, 128], BF16)
        onesD = sb("onesD", [128, 128], BF16)
        onesC = sb("onesC", [128, 128], BF16)
        ones1 = sb("ones1", [128, 128], BF16)
        csb = sb("csb", [128, KC], BF16)
        cw = sb("cw", [128, 2 * FC, 2], F32)
        hhalo = sb("hhalo", [128, KC, 2], BF16)
        upc = [sb(f"upc{i}", [128, 2 * FC, 2], F32) for i in range(2)]
        acar = sb("acar", [128, CC, HALO], BF16)
        identb = sb("identb", [128, 128], BF16)
        dgs = [sb(f"dg{i}", [128, CONV_K, 128], BF16) for i in range(2)]
        bnst = sb("bnst", [128, 4, 12], F32)
        bnmv = sb("bnmv", [128, 4, 4], F32)
        epsT = sb("epsT", [128, 1], F32)
        banks = [st.enter_context(nc.psum_tensor(f"bank{i}", [128, 512], F32)) for i in range(8)]

        def V(name, j=None, n=1):
            o, ln = voff[name]
            if j is None:
                return vecs[:, o:o + ln]
            return vecs[:, o + j:o + j + n]

        MOD = lambda m: dv[:, m:m + 1]
        G1P = lambda k: dv[:, 96 + k:97 + k]
        G2P = lambda k: dv[:, 112 + k:113 + k]

        state = {"bank": 0, "tmp": 0, "slot": 0}

        def dbg(name, ap, reads):
            if not cfg.debug:
                return
            d = nc.dram_tensor("dbg_" + name, list(ap.shape), ap.dtype, kind="ExternalOutput").ap()
            state["ndbg"] = state.get("ndbg", 0) + 1
            P.op("sp", lambda e: e.dma_start(out=d, in_=ap), reads=reads, dma="dbg%d" % state["ndbg"])

        def bank():
            b = state["bank"]
            if b == state.get("skip"):
                b = (b + 1) % 8
            state["bank"] = (b + 1) % 8
            return banks[b], ("ps", b)

        def tmp():
            i = state["tmp"]
            state["tmp"] = (i + 1) % cfg.ntmp
            return tmps[i], ("tmp", i)

        def load_block(src_ap, nelem=SLOT_E):
            s = state["slot"]
            state["slot"] = (s + 1) % cfg.nslot
            dst = slots[s][:, 0:nelem]
            P.op("pool", lambda e, dst=dst, src=src_ap: e.dma_start(out=dst, in_=src),
                 writes=[("slot", s)], dma=f"slot{s}")
            return slots[s], ("slot", s)

        def mm_group(psap, pairs, reads, bankres, n_extra_writes=()):
            def fn(e, psap=psap, pairs=pairs):
                last = None
                n = len(pairs)
                for i, (l, r) in enumerate(pairs):
                    last = e.matmul(psap, lhsT=l, rhs=r, start=(i == 0), stop=(i == n - 1))
                return last
            P.op("pe", fn, reads=reads, writes=[bankres] + list(n_extra_writes))

        P.op("sp", lambda e: e.dma_start(out=vecs[:], in_=vecs_d), writes=["vecs"], dma="cst0")
        P.op("sp", lambda e: e.dma_start(out=bvbc[:], in_=bvbc_d), writes=["bvbc"], dma="cst1")
        P.op("sp", lambda e: e.dma_start(out=maskT[:], in_=mask_d), writes=["mask"], dma="cst2")
        P.op("sp", lambda e: e.dma_start(out=vall[:, 0, :], in_=wsp_d), writes=r_vall(0), dma="cst3")
        P.op("sp", lambda e: e.dma_start(out=vall[:, 1, :], in_=bsp_d), writes=r_vall(1), dma="cst4")
        P.op("dve", lambda e: e.memset(epsT[:], EPS), writes=["epsT"])
        P.op("sp", lambda e: e.dma_start(out=tmps[0][:, 0:128], in_=ident_d), writes=[("tmp", 0)], dma="cst5")
        P.op("dve", lambda e: e.tensor_copy(out=identb[:], in_=tmps[0][:, 0:128]), reads=[("tmp", 0)], writes=["identb"])
        P.op("dve", lambda e: e.memset(onesD[:], 1.0 / D), writes=["onesD"])
        P.op("dve", lambda e: e.memset(onesC[:], 1.0 / DC), writes=["onesC"])
        P.op("dve", lambda e: e.memset(ones1[:], 1.0), writes=["ones1"])
        P.op("dve", lambda e: e.memset(acar[:], 0.0), writes=["acar"])
        P.op("dve", lambda e: e.memset(upc[0][:], 0.0), writes=[("upc", 0)])
        P.op("act", lambda e: e.activation(out=csb[:], in_=V("cT"), func=AF.Silu), reads=["vecs"], writes=["csb"])
        for g in range(CC):
            P.op("dve", lambda e, g=g: e.tensor_tensor(out=wsb[:, g, :], in0=vall[:, 0, g * 128:(g + 1) * 128],
                                                        in1=maskT[:], op=ALU.mult),
                 reads=r_vall(0) + ["mask"], writes=[("wsb", g)])
        for half in range(2):
            bk, bres = bank()
            for gg in range(4):
                g = half * 4 + gg
                mm_group(bk[:, gg * 128:(gg + 1) * 128], [(ones1[:], wsb[:, g, :])],
                         reads=["ones1", ("wsb", g)], bankres=bres)
            for gg in range(4):
                g = half * 4 + gg
                P.op("dve", lambda e, g=g, gg=gg, bk=bk: e.scalar_tensor_tensor(
                    out=bfull[:, g, :], in0=bk[:, gg * 128:(gg + 1) * 128], scalar=V("slb", g),
                    in1=vall[:, 1, g * 128:(g + 1) * 128], op0=ALU.mult, op1=ALU.add),
                    reads=["vecs"] + r_vall(1), writes=[bres, ("bfull", g)])
        o_bada = voff["bada"][0]

        def ada_part(b_lo, b_hi, resname):
            bk_mod, bres_mod = bank()
            for b in range(b_lo, b_hi):
                sl, sres = load_block(w_ada_d[b])
                wv = sl[:, :].rearrange("p (k c) -> p k c", c=256)
                for mm in range(2):
                    m = 2 * b + mm
                    mm_group(bk_mod[:, m:m + 1],
                             [(wv[:, k, mm * 128:(mm + 1) * 128], csb[:, k:k + 1]) for k in range(KC)],
                             reads=[sres, "csb"], bankres=bres_mod)
            m0, m1 = 2 * b_lo, 2 * b_hi
            P.op("dve", lambda e: e.tensor_tensor(out=dv[:, m0:m1], in0=bk_mod[:, m0:m1],
                                                  in1=vecs[:, o_bada + m0:o_bada + m1], op=ALU.add),
                 reads=["vecs"], writes=[bres_mod, resname])

        ada_part(0, 16, "mod")
        P.op("dve", lambda e: e.scalar_tensor_tensor(out=dv[:, 96:112], in0=dv[:, 16:32], scalar=1.0,
                                                      in1=V("n1g"), op0=ALU.add, op1=ALU.mult),
             reads=["mod", "vecs"], writes=["g1p"])
        VEC = ["vecs", "mod", "g1p"]

        def ada_rest():
            ada_part(16, 48, "mod2")
            P.op("dve", lambda e: e.scalar_tensor_tensor(out=dv[:, 112:128], in0=dv[:, 64:80], scalar=1.0,
                                                          in1=V("n2g"), op0=ALU.add, op1=ALU.mult),
                 reads=["mod2", "vecs"], writes=["g2p"])
            VEC.extend(["mod2", "g2p"])

        def rms_to_h(T, gfn, shift_m0, dname=None):
            for k in range(KC):
                P.op("act", lambda e, k=k: e.activation(out=sqm[:, k, :T], in_=xT[:, k, :T], func=AF.Square),
                     reads=[("x", k)], writes=[("sqm", k)])
            bk, bres = bank()
            mm_group(bk[:, :T], [(onesD[:], sqm[:, k, :T]) for k in range(KC)],
                     reads=["onesD"] + [("sqm", k) for k in range(KC)], bankres=bres)
            rstd, rres_ = longs[0], ("L", 0)
            P.op("act", lambda e: e.activation(out=rstd[:, :T], in_=bk[:, :T], func=AF.Sqrt, bias=epsT[:], scale=1.0),
                 reads=["epsT"], writes=[bres, rres_])
            P.op("dve", lambda e: e.reciprocal(out=rstd[:, :T], in_=rstd[:, :T]), reads=[rres_], writes=[rres_])
            if dname is not None:
                dbg(dname + "_sq", sqm[:, :, :T], [("sqm", k) for k in range(KC)])
                dbg(dname + "_rstd", rstd[:, :T], [rres_])
                dbg(dname + "_x", xT[:, :, :T], [("x", k) for k in range(KC)])
            for k in range(KC):
                if gfn is None:
                    continue
                t_, tres = tmp()
                P.op("dve", lambda e, k=k, t_=t_: e.scalar_tensor_tensor(
                    out=t_[:, :T], in0=xT[:, k, :T], scalar=gfn(k), in1=rstd[:, :T], op0=ALU.mult, op1=ALU.mult),
                    reads=[("x", k), rres_] + VEC, writes=[tres])
                P.op("act", lambda e, k=k, t_=t_: e.activation(out=hT[:, k, :T], in_=t_[:, :T], func=AF.Identity,
                                                               bias=MOD(shift_m0 + k), scale=1.0),
                     reads=[tres] + VEC, writes=[("h", k)])
            return rstd, rres_

        HALL = [("h", k) for k in range(KC)]

        def wv256(sl):
            return sl[:, :].rearrange("p (k c) -> p k c", c=256)

        def wv512(sl):
            return sl[:, :].rearrange("p (k c) -> p k c", c=512)

        def do_tile(ti, T, t0):
            halo = (ti == 0)
            nT = T // 128
            cur, nxt = ti % 2, (ti + 1) % 2
            P.op("sp", lambda e, t0=t0, T=T: e.dma_start(out=xT[:, :, :T], in_=xd[:, :, t0:t0 + T]),
                 writes=[("x", k) for k in range(KC)], dma="xin")
            rms_to_h(T, G1P, 0, "n1" if ti == 1 else None)

            if ti == 1:
                dbg("mod", dv[:, :], VEC)
                dbg("h", hT[:, :, :T], HALL)
            P.op("dve", lambda e: e.tensor_copy(out=abuf[:, :, 0:HALO], in_=acar[:]),
                 reads=["acar"], writes=sum([r_a(j) for j in range(CC)], []))
            for jp in range(4):
                slv, rv = load_block(w_in_d[jp])
                slg, rg = load_block(w_in_d[4 + jp])
                for jj in range(2):
                    j = 2 * jp + jj
                    bkv, brv = bank()
                    bkg, brg = bank()
                    mm_group(bkv[:, :T], [(wv256(slv)[:, k, jj * 128:(jj + 1) * 128], hT[:, k, :T]) for k in range(KC)],
                             reads=[rv] + HALL, bankres=brv)
                    mm_group(bkg[:, :T], [(wv256(slg)[:, k, jj * 128:(jj + 1) * 128], hT[:, k, :T]) for k in range(KC)],
                             reads=[rg] + HALL, bankres=brg)
                    sg, sgr = tmp()
                    P.op("act", lambda e, j=j, bkg=bkg, sg=sg: e.activation(
                        out=sg[:, :T], in_=bkg[:, :T], func=AF.Sigmoid, bias=V("b_agate", j), scale=1.0),
                        reads=VEC, writes=[brg, sgr])
                    P.op("dve", lambda e, j=j, bkv=bkv, sg=sg: e.scalar_tensor_tensor(
                        out=abuf[:, j, HALO:HALO + T], in0=bkv[:, :T], scalar=V("b_aval", j), in1=sg[:, :T],
                        op0=ALU.add, op1=ALU.mult), reads=VEC + [sgr], writes=[brv] + r_a(j))

            conv_fill = []

            def conv_chunk(j, T=T, halo=halo):
                o_cdw = voff["cdw"][0]
                wj = vecs[:, o_cdw + j * CONV_K:o_cdw + (j + 1) * CONV_K]
                dg, dgr = dgs[j % 2], ("dg", j % 2)
                P.op("dve", lambda e: e.tensor_tensor(
                    out=dg[:, :, :], in0=identb[:, :].unsqueeze(1).broadcast_to([128, CONV_K, 128]),
                    in1=wj.unsqueeze(2).broadcast_to([128, CONV_K, 128]), op=ALU.mult),
                    reads=VEC + ["identb"], writes=[dgr])
                bkc, brc = bank()
                mm_group(bkc[:, :T], [(dg[:, k, :], abuf[:, j, k:k + T]) for k in range(CONV_K)],
                         reads=[dgr] + r_a(j), bankres=brc)
                P.op("act", lambda e: e.activation(out=acb[:, j, :T], in_=bkc[:, :T], func=AF.Identity,
                                                   bias=V("cdb", j), scale=1.0),
                     reads=VEC, writes=[brc] + r_ac(j))
                if halo:
                    P.op("dve", lambda e: e.tensor_scalar(out=acar[:, j, :], in0=abuf[:, j, T:T + HALO],
                                                          scalar1=V("flag", 0), scalar2=None, op0=ALU.mult),
                         reads=VEC + r_a(j), writes=["acar"])
                else:
                    P.op("dve", lambda e: e.tensor_copy(out=acar[:, j, :], in_=abuf[:, j, T:T + HALO]),
                         reads=r_a(j), writes=["acar"])

            for j in range(CC):
                conv_fill.append(lambda j=j: conv_chunk(j))

            def fill(n=1):
                for _ in range(n):
                    if conv_fill:
                        conv_fill.pop(0)()

            vblk = [load_block(w_in_d[12 + i]) for i in range(4)]
            for n in range(nT):
                bkA, brA = bank()
                bkB, brB = bank()
                for (bk_, br_, b0) in ((bkA, brA, 0), (bkB, brB, 2)):
                    for bb in range(2):
                        sl, sres = vblk[b0 + bb]
                        mm_group(bk_[:, bb * 256:(bb + 1) * 256],
                                 [(hT[:, k, n * 128:(n + 1) * 128], wv256(sl)[:, k, :]) for k in range(KC)],
                                 reads=[sres] + HALL, bankres=br_)
                P.op("dve", lambda e, n=n, bkA=bkA: e.tensor_tensor(out=vall[:, n, 0:512], in0=bkA[:, :],
                                                                    in1=bvbc[:, 0:512], op=ALU.add),
                     reads=["bvbc"], writes=[brA] + r_vall(n))
                P.op("dve", lambda e, n=n, bkB=bkB: e.tensor_tensor(out=vall[:, n, 512:1024], in0=bkB[:, :],
                                                                    in1=bvbc[:, 512:1024], op=ALU.add),
                     reads=["bvbc"], writes=[brB] + r_vall(n))
                P.op("act", lambda e, n=n: e.activation(out=vall[:, n, :], in_=vall[:, n, :], func=AF.Gelu),
                     reads=r_vall(n), writes=r_vall(n))
                P.op("dve", lambda e, n=n: e.bn_stats(out=bnst[:, n, 0:6], in_=vall[:, n, 0:512]),
                     reads=r_vall(n), writes=["bnst"])
                P.op("dve", lambda e, n=n: e.bn_stats(out=bnst[:, n, 6:12], in_=vall[:, n, 512:1024]),
                     reads=r_vall(n), writes=["bnst"])
                P.op("dve", lambda e, n=n: e.bn_aggr(out=bnmv[:, n, 0:2], in_=bnst[:, n, :]),
                     reads=["bnst"], writes=["bnmv"])
            P.op("act", lambda e, nT=nT: e.activation(out=bnmv[:, 0:nT, 2], in_=bnmv[:, 0:nT, 1], func=AF.Sqrt,
                                                      bias=epsT[:], scale=1.0),
                 reads=["bnmv", "epsT"], writes=["bnmv"])
            P.op("dve", lambda e, nT=nT: e.reciprocal(out=bnmv[:, 0:nT, 2], in_=bnmv[:, 0:nT, 2]),
                 reads=["bnmv"], writes=["bnmv"])
            for n in range(nT):
                P.op("dve", lambda e, n=n: e.tensor_scalar(out=vn[:, n, :], in0=vall[:, n, :],
                                                           scalar1=bnmv[:, n, 0:1], scalar2=bnmv[:, n, 2:3],
                                                           op0=ALU.subtract, op1=ALU.mult),
                     reads=r_vall(n) + ["bnmv"], writes=r_vn(n))

            for gp in range(4):
                slu, ru = load_block(w_in_d[8 + gp])
                for gg in range(2):
                    g = 2 * gp + gg
                    bku, bru = bank()
                    bks, brs = bank()
                    mm_group(bku[:, :T], [(wv256(slu)[:, k, gg * 128:(gg + 1) * 128], hT[:, k, :T]) for k in range(KC)],
                             reads=[ru] + HALL, bankres=bru)
                    for n in range(nT):
                        mm_group(bks[:, n * 128:(n + 1) * 128], [(vn[:, n, g * 128:(g + 1) * 128], wsb[:, g, :])],
                                 reads=[("wsb", g)] + r_vn(n), bankres=brs)
                    gu, gur = tmp()
                    P.op("act", lambda e, g=g, bku=bku, gu=gu: e.activation(
                        out=gu[:, :T], in_=bku[:, :T], func=AF.Gelu, bias=V("b_u", g), scale=1.0),
                        reads=VEC, writes=[bru, gur])
                    t2, t2r = tmp()
                    for n in range(nT):
                        P.op("dve", lambda e, g=g, n=n, bks=bks, t2=t2: e.scalar_tensor_tensor(
                            out=t2[:, n * 128:(n + 1) * 128], in0=bks[:, n * 128:(n + 1) * 128], scalar=V("slg", g),
                            in1=bfull[:, g, :], op0=ALU.mult, op1=ALU.add),
                            reads=VEC + [("bfull", g)], writes=[brs, t2r])
                    P.op("dve", lambda e, g=g, t2=t2, gu=gu: e.tensor_tensor(out=ub[:, g, :T], in0=t2[:, :T],
                                                                            in1=gu[:, :T], op=ALU.mult),
                         reads=[t2r, gur], writes=[("ub", g)])
                    fill(1)
            fill(CC)

            if halo:
                ada_rest()
            if ti == 1:
                dbg("abuf", abuf[:, :, 0:HALO + T], sum([r_a(j) for j in range(CC)], []))
                dbg("vn", vn[:, 0:nT, :], sum([r_vn(n) for n in range(nT)], []))
                dbg("ub", ub[:, :, :T], [("ub", k) for k in range(CC)])
                dbg("ac", acb[:, :, :T], sum([r_ac(j) for j in range(CC)], []))
            for j in range(CC):
                P.op("act", lambda e, j=j: e.activation(out=sqm[:, j, :T], in_=acb[:, j, :T], func=AF.Identity),
                     reads=r_ac(j), writes=[("sqm", j)])
                P.op("act", lambda e, j=j: e.activation(out=sqm[:, CC + j, :T], in_=acb[:, j, :T], func=AF.Square),
                     reads=r_ac(j), writes=[("sqm", CC + j)])
            bkm, brm = bank()
            bkq, brq = bank()
            mm_group(bkm[:, :T], [(onesC[:], sqm[:, j, :T]) for j in range(CC)],
                     reads=["onesC"] + [("sqm", j) for j in range(CC)], bankres=brm)
            mm_group(bkq[:, :T], [(onesC[:], sqm[:, CC + j, :T]) for j in range(CC)],
                     reads=["onesC"] + [("sqm", CC + j) for j in range(CC)], bankres=brq)
            mean, meanr = longs[1], ("L", 1)
            P.op("act", lambda e: e.activation(out=mean[:, :T], in_=bkm[:, :T], func=AF.Identity),
                 writes=[brm, meanr])
            m2, m2r = longs[2], ("L", 2)
            P.op("dve", lambda e: e.tensor_tensor(out=m2[:, :T], in0=mean[:, :T], in1=mean[:, :T], op=ALU.mult),
                 reads=[meanr], writes=[m2r])
            P.op("dve", lambda e: e.tensor_tensor(out=m2[:, :T], in0=bkq[:, :T], in1=m2[:, :T], op=ALU.subtract),
                 reads=[m2r], writes=[brq, m2r])
            P.op("act", lambda e: e.activation(out=m2[:, :T], in_=m2[:, :T], func=AF.Sqrt, bias=epsT[:], scale=1.0),
                 reads=[m2r, "epsT"], writes=[m2r])
            P.op("dve", lambda e: e.reciprocal(out=m2[:, :T], in_=m2[:, :T]), reads=[m2r], writes=[m2r])
            for j in range(CC):
                lt, ltr = tmp()
                P.op("dve", lambda e, j=j, lt=lt: e.tensor_tensor(out=lt[:, :T], in0=acb[:, j, :T], in1=mean[:, :T],
                                                                  op=ALU.subtract),
                     reads=r_ac(j) + [meanr], writes=[ltr])
                P.op("dve", lambda e, lt=lt: e.tensor_tensor(out=lt[:, :T], in0=lt[:, :T], in1=m2[:, :T], op=ALU.mult),
                     reads=[ltr, m2r], writes=[ltr])
                P.op("act", lambda e, j=j, lt=lt: e.activation(out=a2[:, j, :T], in_=lt[:, :T], func=AF.Silu,
                                                               bias=V("clb", j), scale=V("clg", j)),
                     reads=[ltr] + VEC, writes=r_a2(j))

            cur_blocks = {}
            for j in range(KC):
                if j % 2 == 0:
                    cur_blocks["ga"] = load_block(w_in_d[16 + j // 2])
                    cur_blocks["gb"] = load_block(w_in_d[24 + j // 2])
                if j % 4 == 0:
                    cur_blocks["so"] = load_block(w_so_d[j // 4])
                    cur_blocks["co"] = load_block(w_co_d[j // 4])
                (sga_, rga), (sgb_, rgb) = cur_blocks["ga"], cur_blocks["gb"]
                (sso, rso), (sco, rco) = cur_blocks["so"], cur_blocks["co"]
                jj = j % 2
                j4 = j % 4
                bkga, brga = bank()
                bkgb, brgb = bank()
                bkyb, bryb = bank()
                bkya, brya = bank()
                mm_group(bkga[:, :T], [(wv256(sga_)[:, k, jj * 128:(jj + 1) * 128], hT[:, k, :T]) for k in range(KC)],
                         reads=[rga] + HALL, bankres=brga)
                mm_group(bkgb[:, :T], [(wv256(sgb_)[:, k, jj * 128:(jj + 1) * 128], hT[:, k, :T]) for k in range(KC)],
                         reads=[rgb] + HALL, bankres=brgb)
                mm_group(bkyb[:, :T], [(wv512(sso)[:, k, j4 * 128:(j4 + 1) * 128], ub[:, k, :T]) for k in range(CC)],
                         reads=[rso] + [("ub", k) for k in range(CC)], bankres=bryb)
                mm_group(bkya[:, :T], [(wv512(sco)[:, k, j4 * 128:(j4 + 1) * 128], a2[:, k, :T]) for k in range(CC)],
                         reads=[rco] + sum([r_a2(k) for k in range(CC)], []), bankres=brya)
                sa, sar = tmp()
                sb_, sbr = tmp()
                P.op("act", lambda e, j=j, bkga=bkga, sa=sa: e.activation(
                    out=sa[:, :T], in_=bkga[:, :T], func=AF.Sigmoid, bias=V("b_ga", j), scale=1.0),
                    reads=VEC, writes=[brga, sar])
                P.op("act", lambda e, j=j, bkgb=bkgb, sb_=sb_: e.activation(
                    out=sb_[:, :T], in_=bkgb[:, :T], func=AF.Sigmoid, bias=V("b_gb", j), scale=1.0),
                    reads=VEC, writes=[brgb, sbr])
                P.op("dve", lambda e, bkyb=bkyb, sb_=sb_: e.tensor_tensor(out=sb_[:, :T], in0=sb_[:, :T], in1=bkyb[:, :T],
                                                                          op=ALU.mult),
                     reads=[sbr], writes=[bryb, sbr])
                P.op("dve", lambda e, bkya=bkya, sa=sa: e.tensor_tensor(out=sa[:, :T], in0=sa[:, :T], in1=bkya[:, :T],
                                                                        op=ALU.mult),
                     reads=[sar], writes=[brya, sar])
                P.op("dve", lambda e, j=j, sa=sa, sb_=sb_: e.tensor_tensor(out=sqm[:, j, :T], in0=sa[:, :T], in1=sb_[:, :T],
                                                                           op=ALU.add),
                     reads=[sar, sbr], writes=[("sqm", j)])

            if ti == 1:
                dbg("a2", a2[:, :, :T], sum([r_a2(j) for j in range(CC)], []))
                dbg("merged", sqm[:, :, :T], [("sqm", k) for k in range(KC)])
            for j in range(KC):
                if j % 2 == 0:
                    swo, rwo = load_block(w_out_d[j // 2])
                jj = j % 2
                bk, br = bank()
                mm_group(bk[:, :T], [(wv256(swo)[:, k, jj * 128:(jj + 1) * 128], sqm[:, k, :T]) for k in range(KC)],
                         reads=[rwo] + [("sqm", k) for k in range(KC)], bankres=br)
                P.op("dve", lambda e, j=j, bk=bk: e.scalar_tensor_tensor(
                    out=xT[:, j, :T], in0=bk[:, :T], scalar=MOD(32 + j), in1=xT[:, j, :T], op0=ALU.mult, op1=ALU.add),
                    reads=VEC + [("x", j)], writes=[br, ("x", j)])

            if ti == 1:
                dbg("x1", xT[:, :, :T], [("x", k) for k in range(KC)])
            rms_to_h(T, G2P, 48)

            def fd(jj, k):
                return V("fdw", jj * 3 + k)

            o_fdw = voff["fdw"][0]
            fdw3 = vecs[:, o_fdw:o_fdw + 2 * FC * 3].rearrange("p (j k) -> p j k", k=3)
            first_own = (ti == 1)
            if halo:
                P.op("dve", lambda e: e.tensor_scalar(out=hhalo[:, :, :], in0=hT[:, :, T - 2:T], scalar1=V("flag", 0),
                                                      scalar2=None, op0=ALU.mult), reads=VEC + HALL, writes=["hh"])
                return
            if first_own:
                bk_small, br_small = bank()
                state["skip"] = int(br_small[1])
            if not first_own:
                P.op("dve", lambda e, cur=cur: e.tensor_tensor(out=cw[:, :, 0], in0=fdw3[:, :, 0], in1=upc[cur][:, :, 0],
                                                               op=ALU.mult), reads=VEC + [("upc", cur)], writes=["cw"])
                P.op("dve", lambda e, cur=cur: e.tensor_tensor(out=cw[:, :, 1], in0=fdw3[:, :, 1], in1=upc[cur][:, :, 1],
                                                               op=ALU.mult), reads=VEC + [("upc", cur)], writes=["cw"])
                P.op("dve", lambda e: e.tensor_tensor(out=cw[:, :, 0], in0=cw[:, :, 0], in1=cw[:, :, 1], op=ALU.add),
                     reads=["cw"], writes=["cw"])
                P.op("dve", lambda e, cur=cur: e.tensor_tensor(out=cw[:, :, 1], in0=fdw3[:, :, 0], in1=upc[cur][:, :, 1],
                                                               op=ALU.mult), reads=VEC + [("upc", cur)], writes=["cw"])
            for jf in range(FC):
                sup, rup = load_block(w_up_d[jf])
                outs = []
                for half, jj in ((0, jf), (1, FC + jf)):
                    if first_own:
                        mm_group(bk_small[:, 2 * jj:2 * jj + 2],
                                 [(wv256(sup)[:, k, half * 128:(half + 1) * 128], hhalo[:, k, :]) for k in range(KC)],
                                 reads=[rup, "hh"], bankres=br_small)
                        P.op("dve", lambda e, jj=jj: e.tensor_scalar(
                            out=cw[:, jj, 0:1], in0=bk_small[:, 2 * jj:2 * jj + 1], scalar1=fd(jj, 0), scalar2=None,
                            op0=ALU.mult), reads=VEC, writes=[br_small, "cw"])
                        P.op("dve", lambda e, jj=jj: e.scalar_tensor_tensor(
                            out=cw[:, jj, 0:1], in0=bk_small[:, 2 * jj + 1:2 * jj + 2], scalar=fd(jj, 1),
                            in1=cw[:, jj, 0:1], op0=ALU.mult, op1=ALU.add), reads=VEC + ["cw"], writes=[br_small, "cw"])
                        P.op("dve", lambda e, jj=jj: e.tensor_scalar(
                            out=cw[:, jj, 1:2], in0=bk_small[:, 2 * jj + 1:2 * jj + 2], scalar1=fd(jj, 0), scalar2=None,
                            op0=ALU.mult), reads=VEC, writes=[br_small, "cw"])
                    bk, br = bank()
                    mm_group(bk[:, :T], [(wv256(sup)[:, k, half * 128:(half + 1) * 128], hT[:, k, :T]) for k in range(KC)],
                             reads=[rup] + HALL, bankres=br)
                    if halo:
                        P.op("dve", lambda e, jj=jj, bk=bk, nxt=nxt: e.tensor_scalar(
                            out=upc[nxt][:, jj, :], in0=bk[:, T - 2:T], scalar1=V("flag", 0), scalar2=None, op0=ALU.mult),
                            reads=VEC, writes=[br, ("upc", nxt)])
                        continue
                    o_, orr = tmp()
                    P.op("act", lambda e, jj=jj, bk=bk, o_=o_: e.activation(
                        out=o_[:, :T], in_=bk[:, :T], func=AF.Identity, bias=V("fdb", jj), scale=fd(jj, 2)),
                        reads=VEC, writes=[br, orr])
                    P.op("dve", lambda e, jj=jj, bk=bk, o_=o_: e.scalar_tensor_tensor(
                        out=o_[:, 1:T], in0=bk[:, 0:T - 1], scalar=fd(jj, 1), in1=o_[:, 1:T], op0=ALU.mult, op1=ALU.add),
                        reads=VEC + [orr], writes=[br, orr])
                    P.op("dve", lambda e, jj=jj, bk=bk, o_=o_: e.scalar_tensor_tensor(
                        out=o_[:, 2:T], in0=bk[:, 0:T - 2], scalar=fd(jj, 0), in1=o_[:, 2:T], op0=ALU.mult, op1=ALU.add),
                        reads=VEC + [orr], writes=[br, orr])
                    P.op("dve", lambda e, jj=jj, o_=o_: e.tensor_tensor(out=o_[:, 0:2], in0=o_[:, 0:2], in1=cw[:, jj, :],
                                                                        op=ALU.add),
                         reads=["cw", orr], writes=[orr])
                    P.op("dve", lambda e, jj=jj, bk=bk, nxt=nxt: e.tensor_copy(out=upc[nxt][:, jj, :], in_=bk[:, T - 2:T]),
                         writes=[br, ("upc", nxt)])
                    outs.append((o_, orr))
                if halo:
                    continue
                (ov, ovr), (og, ogr) = outs
                P.op("act", lambda e, og=og: e.activation(out=og[:, :T], in_=og[:, :T], func=AF.Silu),
                     reads=[ogr], writes=[ogr])
                P.op("dve", lambda e, jf=jf, ov=ov, og=og: e.tensor_tensor(out=gbuf[:, jf, :T], in0=og[:, :T], in1=ov[:, :T],
                                                                           op=ALU.mult),
                     reads=[ovr, ogr], writes=r_g(jf))

            state["skip"] = None
            if ti == 1:
                dbg("g", gbuf[:, :, :T], sum([r_g(k) for k in range(FC)], []))
                dbg("cw", cw[:, :, :], ["cw"])
            if not halo:
                for j in range(KC):
                    sd0, rd0 = load_block(w_dn_d[2 * j], HK * 128)
                    sd1, rd1 = load_block(w_dn_d[2 * j + 1], HK * 128)
                    bk, br = bank()
                    pairs = []
                    for hh, sd in ((0, sd0), (1, sd1)):
                        wv = sd[:, 0:HK * 128].rearrange("p (k c) -> p k c", c=128)
                        for kk in range(HK):
                            pairs.append((wv[:, kk, :], gbuf[:, hh * HK + kk, :T]))
                    mm_group(bk[:, :T], pairs, reads=[rd0, rd1] + sum([r_g(k) for k in range(FC)], []), bankres=br)
                    P.op("dve", lambda e, j=j, bk=bk: e.scalar_tensor_tensor(
                        out=xT[:, j, :T], in0=bk[:, :T], scalar=MOD(80 + j), in1=xT[:, j, :T], op0=ALU.mult, op1=ALU.add),
                        reads=VEC + [("x", j)], writes=[br, ("x", j)])
                if ti == 1:
                    dbg("x2", xT[:, :, :T], [("x", k) for k in range(KC)])
                rstd, rres_ = rms_to_h(T, None, 0)
                for k in range(KC):
                    P.op("dve", lambda e, k=k, rstd=rstd: e.scalar_tensor_tensor(
                        out=ystage[:, k, :T], in0=xT[:, k, :T], scalar=V("fg", k), in1=rstd[:, :T], op0=ALU.mult, op1=ALU.mult),
                        reads=VEC + [("x", k), rres_], writes=rres(k * TM, (k + 1) * TM))
                P.op("sp", lambda e, t0=t0, T=T: e.dma_start(out=yd[:, :, t0 - 128:t0 - 128 + T], in_=ystage[:, :, :T]),
                     reads=rres(0, KC * TM), dma="yout")

        t0_ = 0
        for ti_, T_ in enumerate(cfg.tiles):
            do_tile(ti_, T_, t0_)
            t0_ += T_

        n_out = P.dcnt.get("yout", 0)

        names = P.sem_names()
        sems = {n: st.enter_context(nc.semaphore(n)) for n in names}
        block = st.enter_context(nc.Block())

        @block.tensor
        def _(e):
            P.emit("pe", e, sems)

        @block.scalar
        def _(e):
            P.emit("act", e, sems)

        @block.vector
        def _(e):
            P.emit("dve", e, sems)

        @block.gpsimd
        def _(e):
            P.emit("pool", e, sems)

        @block.sync
        def _(e):
            P.emit("sp", e, sems)
            e.wait_ge(sems["d_yout"], 16 * n_out)
            for i in range(1, state.get("ndbg", 0) + 1):
                e.wait_ge(sems["d_dbg%d" % i], 16)
    return nc


def _blk_k(W, kc, cols):
    K, N = W.shape
    nb = N // cols
    return np.ascontiguousarray(W.reshape(kc, 128, nb, cols).transpose(2, 1, 0, 3).reshape(nb, 128, kc * cols))


def _fm(v, n):
    return np.asarray(v, np.float32).reshape(n, 128).T


def prep_shared(cfg, inp):
    FC, HK, DFF = cfg.FC, cfg.HK, cfg.DFF
    f = lambda a: np.asarray(a, np.float32)
    sh = {}
    sh["w_ada"] = _blk_k(f(inp["w_ada"])[0], KC, 256)
    sh["w_in"] = _blk_k(f(inp["w_in"])[0], KC, 256)
    sh["w_co"] = _blk_k(f(inp["w_conv_out"])[0], CC, 512)
    sh["w_so"] = _blk_k(f(inp["w_sgu_out"])[0], CC, 512)
    sh["w_out"] = _blk_k(f(inp["w_out"])[0], KC, 256)
    wu = f(inp["w_up"])[0]
    wu2 = np.concatenate([wu[:, :DFF].reshape(D, FC, 1, 128), wu[:, DFF:].reshape(D, FC, 1, 128)], axis=2)
    sh["w_up"] = _blk_k(wu2.reshape(D, FC * 256), KC, 256)
    wd = f(inp["w_down"])[0]
    sh["w_dn"] = np.ascontiguousarray(
        wd.reshape(2, HK, 128, KC, 128).transpose(3, 0, 2, 1, 4).reshape(2 * KC, 128, HK * 128))
    b_in = f(inp["b_in"])[0]
    sh["bvbc"] = np.ascontiguousarray(np.broadcast_to(b_in[3072:4096][None, :], (128, 1024)))
    sh["bsp"] = np.ascontiguousarray(np.broadcast_to(f(inp["b_spatial"])[0].reshape(1, 1024), (128, 1024)))
    sh["wspT"] = np.ascontiguousarray(f(inp["w_spatial"])[0].transpose(2, 0, 1).reshape(128, 1024))
    sh["maskT"] = np.ascontiguousarray(np.triu(np.ones((128, 128), np.float32)))
    sh["ident"] = np.ascontiguousarray(np.eye(128, dtype=np.float32))
    voff, NV = vec_layout(cfg)
    vecs = np.zeros((128, NV), np.float32)

    def put(name, arr):
        o, n = voff[name]
        assert arr.shape == (128, n), (name, arr.shape, n)
        vecs[:, o:o + n] = arr

    put("n1g", _fm(f(inp["norm1_g"])[0], KC))
    put("n2g", _fm(f(inp["norm2_g"])[0], KC))
    put("fg", _fm(f(inp["final_g"]), KC))
    put("b_aval", _fm(b_in[0:1024], CC))
    put("b_agate", _fm(b_in[1024:2048], CC))
    put("b_u", _fm(b_in[2048:3072], CC))
    put("b_ga", _fm(b_in[4096:6144], KC))
    put("b_gb", _fm(b_in[6144:8192], KC))
    put("cdw", np.ascontiguousarray(f(inp["conv_dw_w"])[0].reshape(CONV_K, CC, 128).transpose(2, 1, 0)).reshape(128, CC * CONV_K))
    put("cdb", _fm(f(inp["conv_dw_b"])[0], CC))
    put("clg", _fm(f(inp["conv_ln_g"])[0], CC))
    put("clb", _fm(f(inp["conv_ln_b"])[0], CC))
    put("slg", _fm(f(inp["sgu_ln_g"])[0], CC))
    put("slb", _fm(f(inp["sgu_ln_b"])[0], CC))
    put("fdw", np.ascontiguousarray(f(inp["ffn_dw_w"])[0].reshape(3, 2 * FC, 128).transpose(2, 1, 0)).reshape(128, 2 * FC * 3))
    put("fdb", _fm(f(inp["ffn_dw_b"])[0], 2 * FC))
    put("bada", _fm(f(inp["b_ada"])[0], 96))
    sh["vecs"] = vecs
    return sh, voff


def make_in_maps(cfg, inp):
    x = np.asarray(inp["x"], np.float32)
    c = np.asarray(inp["c"], np.float32)
    B, S, _ = x.shape
    per_seq = NCORES // B
    own = cfg.NOUT
    assert per_seq * own == S
    sh, voff = prep_shared(cfg, inp)
    in_maps = []
    for core in range(NCORES):
        b, q = divmod(core, per_seq)
        s0 = q * own
        xc = np.zeros((128 + own, D), np.float32)
        xc[128:] = x[b, s0:s0 + own]
        if q > 0:
            xc[:128] = x[b, s0 - 128:s0]
        xT = np.ascontiguousarray(xc.reshape(128 + own, KC, 128).transpose(2, 1, 0))
        vecs = sh["vecs"].copy()
        o, n = voff["flag"]
        vecs[:, o] = 1.0 if q > 0 else 0.0
        o, n = voff["cT"]
        vecs[:, o:o + n] = _fm(c[b], KC)
        m = {k: v for k, v in sh.items() if k != "vecs"}
        m["vecs"] = vecs
        m["xT"] = xT
        in_maps.append(m)
    return in_maps


def kernel(**inp):
    cfg = CFG
    x = np.asarray(inp["x"], np.float32)
    B, S, _ = x.shape
    per_seq = NCORES // B
    own = cfg.NOUT
    in_maps = make_in_maps(cfg, inp)
    nc = build_program(cfg)
    res = run_bass_kernel_spmd(nc, in_maps, core_ids=list(range(NCORES)))
    global LAST_RES
    LAST_RES = res.results
    out = np.empty((B, S, D), np.float32)
    for core in range(NCORES):
        b, q = divmod(core, per_seq)
        yT = np.asarray(res.results[core]["yT"])
        out[b, q * own:(q + 1) * own] = yT.transpose(2, 1, 0).reshape(own, D)
    return out
```

```python
import numpy as np
from contextlib import ExitStack
import concourse.bass as bass
import concourse.mybir as mybir
from concourse.bass_utils import run_bass_kernel_spmd

F32 = mybir.dt.float32
BF16 = mybir.dt.bfloat16
AF = mybir.ActivationFunctionType
ALU = mybir.AluOpType

D = 2048
KC = 16
DC = 1024
CC = 8
CONV_K = 31
HALO = CONV_K - 1
EPS = 1e-6
NCORES = 8
SLOT_E = 4096


class Cfg:
    def __init__(self, dff=5632, tiles=(128, 512, 512, 512, 512), nslot=5, ntmp=5):
        self.DFF = dff
        self.FC = dff // 128
        self.HK = self.FC // 2
        self.tiles = tuple(tiles)
        self.TM = max(tiles)
        self.NTOK = sum(tiles)
        self.NOUT = self.NTOK - 128
        self.nslot = nslot
        self.ntmp = ntmp
        self.debug = False


CFG = Cfg()


def vec_layout(cfg):
    FC = cfg.FC
    items = [("n1g", KC), ("n2g", KC), ("fg", KC), ("b_aval", CC), ("b_agate", CC), ("b_u", CC),
             ("b_ga", KC), ("b_gb", KC), ("cdw", CC * CONV_K), ("cdb", CC), ("clg", CC), ("clb", CC),
             ("slg", CC), ("slb", CC), ("fdw", 2 * FC * 3), ("fdb", 2 * FC), ("bada", 96),
             ("flag", 1), ("cT", KC)]
    off = {}
    o = 0
    for name, n in items:
        off[name] = (o, n)
        o += n
    return off, o


class Prog:
    ENG = ("pe", "act", "dve", "pool", "sp")

    def __init__(self):
        self.q = {e: [] for e in self.ENG}
        self.cnt = {e: 0 for e in self.ENG}
        self.waited = {e: {} for e in self.ENG}
        self.lastw = {}
        self.readers = {}
        self.dcnt = {}

    def op(self, eng, fn, reads=(), writes=(), dma=None):
        deps = {}
        for r in reads:
            t = self.lastw.get(r)
            if t is not None:
                deps[t] = True
        for w in writes:
            t = self.lastw.get(w)
            if t is not None:
                deps.setdefault(t, False)
            for t in self.readers.get(w, ()):
                deps.setdefault(t, False)
        if dma is not None:
            n = self.dcnt.get(dma, 0) + 1
            self.dcnt[dma] = n
            tok = ("d_" + dma, 16 * n, None)
        else:
            self.cnt[eng] += 1
            tok = ("e_" + eng, self.cnt[eng], eng)
        wd = self.waited[eng]
        need = {}
        for (sem, val, te), raw in deps.items():
            if te == eng and eng == "pe":
                continue
            if wd.get(sem, 0) >= val:
                continue
            if need.get(sem, 0) < val:
                need[sem] = val
        for sem, val in need.items():
            wd[sem] = val
        self.q[eng].append((list(need.items()), fn, (tok[0], 16 if dma is not None else 1)))
        for r in reads:
            self.readers.setdefault(r, []).append(tok)
        for w in writes:
            self.lastw[w] = tok
            self.readers[w] = []
        return tok

    def sem_names(self):
        s = set()
        for e in self.ENG:
            for waits, _, inc in self.q[e]:
                s.add(inc[0])
                for sem, _ in waits:
                    s.add(sem)
        return sorted(s)

    def emit(self, eng, handle, sems):
        for waits, fn, inc in self.q[eng]:
            for sem, val in waits:
                handle.wait_ge(sems[sem], val)
            ins = fn(handle)
            ins.then_inc(sems[inc[0]], inc[1])


def build_program(cfg):
    FC, HK, TM, NTOK, NOUT = cfg.FC, cfg.HK, cfg.TM, cfg.NTOK, cfg.NOUT
    nc = bass.Bass("TRN2", target_bir_lowering=False)
    P = Prog()
    voff, NV = vec_layout(cfg)

    def din(name, shape):
        return nc.dram_tensor(name, list(shape), F32, kind="ExternalInput").ap()

    xd = din("xT", [128, KC, NTOK])
    vecs_d = din("vecs", [128, NV])
    bvbc_d = din("bvbc", [128, 1024])
    bsp_d = din("bsp", [128, 1024])
    wsp_d = din("wspT", [128, 1024])
    mask_d = din("maskT", [128, 128])
    ident_d = din("ident", [128, 128])
    w_ada_d = din("w_ada", [48, 128, SLOT_E])
    w_in_d = din("w_in", [32, 128, SLOT_E])
    w_co_d = din("w_co", [8, 128, SLOT_E // 2])
    w_so_d = din("w_so", [8, 128, SLOT_E // 2])
    w_out_d = din("w_out", [8, 128, SLOT_E])
    w_up_d = din("w_up", [FC, 128, SLOT_E])
    w_dn_d = din("w_dn", [32, 128, HK * 128])
    yd = nc.dram_tensor("yT", [128, KC, NOUT], F32, kind="ExternalOutput").ap()

    with ExitStack() as st:
        def sb(name, shape, dt):
            return st.enter_context(nc.sbuf_tensor(name, list(shape), dt))

        xT = sb("xTs", [128, KC, TM], F32)
        hT = sb("hT", [128, KC, TM], BF16)
        sqm = sb("sqm", [128, KC, TM], BF16)
        ub = sb("ub", [128, CC, TM], BF16)
        AW = HALO + TM
        AWW = CC * AW // 2
        o_a, o_ac, o_a2, o_vn = 0, AWW, AWW + CC * TM, AWW + CC * TM + CC * TM // 2
        RW = max(o_vn + 4 * 1024 // 2, FC * TM // 2)
        R = sb("R", [128, RW], F32)
        abuf = R[:, o_a:o_a + AWW].bitcast(BF16).rearrange("p (c t) -> p c t", t=AW)
        acb = R[:, o_ac:o_ac + CC * TM].rearrange("p (c t) -> p c t", t=TM)
        a2 = R[:, o_a2:o_a2 + CC * TM // 2].bitcast(BF16).rearrange("p (c t) -> p c t", t=TM)
        vn = R[:, o_vn:o_vn + 2048].bitcast(BF16).rearrange("p (n c) -> p n c", c=1024)
        gbuf = R[:, 0:FC * TM // 2].bitcast(BF16).rearrange("p (c t) -> p c t", t=TM)
        vall = R[:, o_ac:o_ac + 4096].rearrange("p (n c) -> p n c", c=1024)
        ystage = R[:, 0:KC * TM].rearrange("p (c t) -> p c t", t=TM)
        CELL = 256

        def rres(lo, hi):
            return [("R", c) for c in range(lo // CELL, (hi - 1) // CELL + 1)]

        def r_a(j):
            return rres(o_a + j * AW // 2, o_a + (j + 1) * AW // 2)

        def r_ac(j):
            return rres(o_ac + j * TM, o_ac + (j + 1) * TM)

        def r_a2(j):
            return rres(o_a2 + j * TM // 2, o_a2 + (j + 1) * TM // 2)

        def r_vn(n):
            return rres(o_vn + n * 512, o_vn + (n + 1) * 512)

        def r_vall(n):
            return rres(o_ac + n * 1024, o_ac + (n + 1) * 1024)

        def r_g(j):
            return rres(j * TM // 2, (j + 1) * TM // 2)

        slots = [sb(f"slot{i}", [128, SLOT_E], BF16) for i in range(cfg.nslot)]
        tmps = [sb(f"tmp{i}", [128, TM], F32) for i in range(cfg.ntmp)]
        longs = [sb(f"long{i}", [128, TM], F32) for i in range(3)]
        vecs = sb("vecs_s", [128, NV], F32)
        dv = sb("dv", [128, 96 + 2 * KC], F32)
        bvbc = sb("bvbc_s", [128, 1024], F32)
        bfull = sb("bfull", [128, CC, 128], F32)
        maskT = sb("mask_s", [128, 128], F32)
        wsb = sb("wsb", [128, CC, 128], BF16)
        onesD = sb("onesD", [128, 128], BF16)
        onesC = sb("onesC", [128, 128], BF16)
        ones1 = sb("ones1", [128, 128], BF16)
        csb = sb("csb", [128, KC], BF16)
        cw = sb("cw", [128, 2 * FC, 2], F32)
        hhalo = sb("hhalo", [128, KC, 2], BF16)
        upc = [sb(f"upc{i}", [128, 2 * FC, 2], F32) for i in range(2)]
        acar = sb("acar", [128, CC, HALO], BF16)
        identb = sb("identb", [128, 128], BF16)
        dgs = [sb(f"dg{i}", [128, CONV_K, 128], BF16) for i in range(2)]
        bnst = sb("bnst", [128, 4, 12], F32)
        bnmv = sb("bnmv", [128, 4, 4], F32)
        epsT = sb("epsT", [128, 1], F32)
        banks = [st.enter_context(nc.psum_tensor(f"bank{i}", [128, 512], F32)) for i in range(8)]

        def V(name, j=None, n=1):
            o, ln = voff[name]
            if j is None:
                return vecs[:, o:o + ln]
            return vecs[:, o + j:o + j + n]

        MOD = lambda m: dv[:, m:m + 1]
        G1P = lambda k: dv[:, 96 + k:97 + k]
        G2P = lambda k: dv[:, 112 + k:113 + k]

        state = {"bank": 0, "tmp": 0, "slot": 0}

        def dbg(name, ap, reads):
            if not cfg.debug:
                return
            d = nc.dram_tensor("dbg_" + name, list(ap.shape), ap.dtype, kind="ExternalOutput").ap()
            state["ndbg"] = state.get("ndbg", 0) + 1
            P.op("sp", lambda e: e.dma_start(out=d, in_=ap), reads=reads, dma="dbg%d" % state["ndbg"])

        def bank():
            b = state["bank"]
            if b == state.get("skip"):
                b = (b + 1) % 8
            state["bank"] = (b + 1) % 8
            return banks[b], ("ps", b)

        def tmp():
            i = state["tmp"]
            state["tmp"] = (i + 1) % cfg.ntmp
            return tmps[i], ("tmp", i)

        def load_block(src_ap, nelem=SLOT_E):
            s = state["slot"]
            state["slot"] = (s + 1) % cfg.nslot
            dst = slots[s][:, 0:nelem]
            P.op("pool", lambda e, dst=dst, src=src_ap: e.dma_start(out=dst, in_=src),
                 writes=[("slot", s)], dma=f"slot{s}")
            return slots[s], ("slot", s)

        def load_pair(src_a, src_b):
            s = state["slot"]
            state["slot"] = (s + 1) % cfg.nslot
            H = SLOT_E // 2
            for i, src in enumerate((src_a, src_b)):
                dst = slots[s][:, i * H:(i + 1) * H]
                P.op("pool", lambda e, dst=dst, src=src: e.dma_start(out=dst, in_=src),
                     writes=[("slot", s)], dma=f"slot{s}")
            return slots[s], ("slot", s)

        def mm_group(psap, pairs, reads, bankres, n_extra_writes=()):
            def fn(e, psap=psap, pairs=pairs):
                last = None
                n = len(pairs)
                for i, (l, r) in enumerate(pairs):
                    last = e.matmul(psap, lhsT=l, rhs=r, start=(i == 0), stop=(i == n - 1))
                return last
            P.op("pe", fn, reads=reads, writes=[bankres] + list(n_extra_writes))

        P.op("sp", lambda e: e.dma_start(out=vecs[:], in_=vecs_d), writes=["vecs"], dma="cst0")
        P.op("sp", lambda e: e.dma_start(out=bvbc[:], in_=bvbc_d), writes=["bvbc"], dma="cst1")
        P.op("sp", lambda e: e.dma_start(out=maskT[:], in_=mask_d), writes=["mask"], dma="cst2")
        P.op("sp", lambda e: e.dma_start(out=vall[:, 0, :], in_=wsp_d), writes=r_vall(0), dma="cst3")
        P.op("sp", lambda e: e.dma_start(out=vall[:, 1, :], in_=bsp_d), writes=r_vall(1), dma="cst4")
        P.op("dve", lambda e: e.memset(epsT[:], EPS), writes=["epsT"])
        P.op("sp", lambda e: e.dma_start(out=tmps[0][:, 0:128], in_=ident_d), writes=[("tmp", 0)], dma="cst5")
        P.op("dve", lambda e: e.tensor_copy(out=identb[:], in_=tmps[0][:, 0:128]), reads=[("tmp", 0)], writes=["identb"])
        P.op("dve", lambda e: e.memset(onesD[:], 1.0 / D), writes=["onesD"])
        P.op("dve", lambda e: e.memset(onesC[:], 1.0 / DC), writes=["onesC"])
        P.op("dve", lambda e: e.memset(ones1[:], 1.0), writes=["ones1"])
        P.op("dve", lambda e: e.memset(acar[:], 0.0), writes=["acar"])
        P.op("dve", lambda e: e.memset(upc[0][:], 0.0), writes=[("upc", 0)])
        P.op("act", lambda e: e.activation(out=csb[:], in_=V("cT"), func=AF.Silu), reads=["vecs"], writes=["csb"])
        for g in range(CC):
            P.op("dve", lambda e, g=g: e.tensor_tensor(out=wsb[:, g, :], in0=vall[:, 0, g * 128:(g + 1) * 128],
                                                        in1=maskT[:], op=ALU.mult),
                 reads=r_vall(0) + ["mask"], writes=[("wsb", g)])
        for half in range(2):
            bk, bres = bank()
            for gg in range(4):
                g = half * 4 + gg
                mm_group(bk[:, gg * 128:(gg + 1) * 128], [(ones1[:], wsb[:, g, :])],
                         reads=["ones1", ("wsb", g)], bankres=bres)
            for gg in range(4):
                g = half * 4 + gg
                P.op("dve", lambda e, g=g, gg=gg, bk=bk: e.scalar_tensor_tensor(
                    out=bfull[:, g, :], in0=bk[:, gg * 128:(gg + 1) * 128], scalar=V("slb", g),
                    in1=vall[:, 1, g * 128:(g + 1) * 128], op0=ALU.mult, op1=ALU.add),
                    reads=["vecs"] + r_vall(1), writes=[bres, ("bfull", g)])
        o_bada = voff["bada"][0]

        def ada_part(b_lo, b_hi, resname):
            bk_mod, bres_mod = bank()
            for b in range(b_lo, b_hi):
                sl, sres = load_block(w_ada_d[b])
                wv = sl[:, :].rearrange("p (k c) -> p k c", c=256)
                for mm in range(2):
                    m = 2 * b + mm
                    mm_group(bk_mod[:, m:m + 1],
                             [(wv[:, k, mm * 128:(mm + 1) * 128], csb[:, k:k + 1]) for k in range(KC)],
                             reads=[sres, "csb"], bankres=bres_mod)
            m0, m1 = 2 * b_lo, 2 * b_hi
            P.op("dve", lambda e: e.tensor_tensor(out=dv[:, m0:m1], in0=bk_mod[:, m0:m1],
                                                  in1=vecs[:, o_bada + m0:o_bada + m1], op=ALU.add),
                 reads=["vecs"], writes=[bres_mod, resname])

        ada_part(0, 16, "mod")
        P.op("dve", lambda e: e.scalar_tensor_tensor(out=dv[:, 96:112], in0=dv[:, 16:32], scalar=1.0,
                                                      in1=V("n1g"), op0=ALU.add, op1=ALU.mult),
             reads=["mod", "vecs"], writes=["g1p"])
        VEC = ["vecs", "mod", "g1p"]

        def ada_rest():
            ada_part(16, 48, "mod2")
            P.op("dve", lambda e: e.scalar_tensor_tensor(out=dv[:, 112:128], in0=dv[:, 64:80], scalar=1.0,
                                                          in1=V("n2g"), op0=ALU.add, op1=ALU.mult),
                 reads=["mod2", "vecs"], writes=["g2p"])
            VEC.extend(["mod2", "g2p"])

        def rms_to_h(T, gfn, shift_m0, dname=None):
            for k in range(KC):
                P.op("act", lambda e, k=k: e.activation(out=sqm[:, k, :T], in_=xT[:, k, :T], func=AF.Square),
                     reads=[("x", k)], writes=[("sqm", k)])
            bk, bres = bank()
            mm_group(bk[:, :T], [(onesD[:], sqm[:, k, :T]) for k in range(KC)],
                     reads=["onesD"] + [("sqm", k) for k in range(KC)], bankres=bres)
            rstd, rres_ = longs[0], ("L", 0)
            P.op("act", lambda e: e.activation(out=rstd[:, :T], in_=bk[:, :T], func=AF.Sqrt, bias=epsT[:], scale=1.0),
                 reads=["epsT"], writes=[bres, rres_])
            P.op("dve", lambda e: e.reciprocal(out=rstd[:, :T], in_=rstd[:, :T]), reads=[rres_], writes=[rres_])
            if dname is not None:
                dbg(dname + "_sq", sqm[:, :, :T], [("sqm", k) for k in range(KC)])
                dbg(dname + "_rstd", rstd[:, :T], [rres_])
                dbg(dname + "_x", xT[:, :, :T], [("x", k) for k in range(KC)])
            for k in range(KC):
                if gfn is None:
                    continue
                t_, tres = tmp()
                P.op("dve", lambda e, k=k, t_=t_: e.scalar_tensor_tensor(
                    out=t_[:, :T], in0=xT[:, k, :T], scalar=gfn(k), in1=rstd[:, :T], op0=ALU.mult, op1=ALU.mult),
                    reads=[("x", k), rres_] + VEC, writes=[tres])
                P.op("act", lambda e, k=k, t_=t_: e.activation(out=hT[:, k, :T], in_=t_[:, :T], func=AF.Identity,
                                                               bias=MOD(shift_m0 + k), scale=1.0),
                     reads=[tres] + VEC, writes=[("h", k)])
            return rstd, rres_

        HALL = [("h", k) for k in range(KC)]

        def wv256(sl):
            return sl[:, :].rearrange("p (k c) -> p k c", c=256)

        def wv512(sl):
            return sl[:, :].rearrange("p (k c) -> p k c", c=512)

        def do_tile(ti, T, t0):
            halo = (ti == 0)
            nT = T // 128
            cur, nxt = ti % 2, (ti + 1) % 2
            P.op("sp", lambda e, t0=t0, T=T: e.dma_start(out=xT[:, :, :T], in_=xd[:, :, t0:t0 + T]),
                 writes=[("x", k) for k in range(KC)], dma="xin")
            rms_to_h(T, G1P, 0, "n1" if ti == 1 else None)

            if ti == 1:
                dbg("mod", dv[:, :], VEC)
                dbg("h", hT[:, :, :T], HALL)
            P.op("dve", lambda e: e.tensor_copy(out=abuf[:, :, 0:HALO], in_=acar[:]),
                 reads=["acar"], writes=sum([r_a(j) for j in range(CC)], []))
            for jp in range(4):
                slv, rv = load_block(w_in_d[jp])
                slg, rg = load_block(w_in_d[4 + jp])
                for jj in range(2):
                    j = 2 * jp + jj
                    bkv, brv = bank()
                    bkg, brg = bank()
                    mm_group(bkv[:, :T], [(wv256(slv)[:, k, jj * 128:(jj + 1) * 128], hT[:, k, :T]) for k in range(KC)],
                             reads=[rv] + HALL, bankres=brv)
                    mm_group(bkg[:, :T], [(wv256(slg)[:, k, jj * 128:(jj + 1) * 128], hT[:, k, :T]) for k in range(KC)],
                             reads=[rg] + HALL, bankres=brg)
                    sg, sgr = tmp()
                    P.op("act", lambda e, j=j, bkg=bkg, sg=sg: e.activation(
                        out=sg[:, :T], in_=bkg[:, :T], func=AF.Sigmoid, bias=V("b_agate", j), scale=1.0),
                        reads=VEC, writes=[brg, sgr])
                    P.op("dve", lambda e, j=j, bkv=bkv, sg=sg: e.scalar_tensor_tensor(
                        out=abuf[:, j, HALO:HALO + T], in0=bkv[:, :T], scalar=V("b_aval", j), in1=sg[:, :T],
                        op0=ALU.add, op1=ALU.mult), reads=VEC + [sgr], writes=[brv] + r_a(j))

            conv_fill = []

            def conv_build(j):
                o_cdw = voff["cdw"][0]
                wj = vecs[:, o_cdw + j * CONV_K:o_cdw + (j + 1) * CONV_K]
                dg, dgr = dgs[j % 2], ("dg", j % 2)
                P.op("dve", lambda e: e.tensor_tensor(
                    out=dg[:, :, :], in0=identb[:, :].unsqueeze(1).broadcast_to([128, CONV_K, 128]),
                    in1=wj.unsqueeze(2).broadcast_to([128, CONV_K, 128]), op=ALU.mult),
                    reads=VEC + ["identb"], writes=[dgr])

            def conv_chunk(j, T=T, halo=halo):
                dg, dgr = dgs[j % 2], ("dg", j % 2)
                bkc, brc = bank()
                mm_group(bkc[:, :T], [(dg[:, k, :], abuf[:, j, k:k + T]) for k in range(CONV_K)],
                         reads=[dgr] + r_a(j), bankres=brc)
                P.op("act", lambda e: e.activation(out=acb[:, j, :T], in_=bkc[:, :T], func=AF.Identity,
                                                   bias=V("cdb", j), scale=1.0),
                     reads=VEC, writes=[brc] + r_ac(j))
                if halo:
                    P.op("dve", lambda e: e.tensor_scalar(out=acar[:, j, :], in0=abuf[:, j, T:T + HALO],
                                                          scalar1=V("flag", 0), scalar2=None, op0=ALU.mult),
                         reads=VEC + r_a(j), writes=["acar"])
                else:
                    P.op("dve", lambda e: e.tensor_copy(out=acar[:, j, :], in_=abuf[:, j, T:T + HALO]),
                         reads=r_a(j), writes=["acar"])

            for j in range(CC):
                conv_fill.append(lambda j=j: conv_chunk(j))

            def fill(n=1):
                for _ in range(n):
                    if conv_fill:
                        conv_fill.pop(0)()

            for vh in range(2):
                vblk = [load_block(w_in_d[12 + 2 * vh + i]) for i in range(2)]
                for n in range(nT):
                    bk_, br_ = bank()
                    for bb in range(2):
                        sl, sres = vblk[bb]
                        mm_group(bk_[:, bb * 256:(bb + 1) * 256],
                                 [(hT[:, k, n * 128:(n + 1) * 128], wv256(sl)[:, k, :]) for k in range(KC)],
                                 reads=[sres] + HALL, bankres=br_)
                    P.op("dve", lambda e, n=n, vh=vh, bk_=bk_: e.tensor_tensor(
                        out=vall[:, n, vh * 512:(vh + 1) * 512], in0=bk_[:, :], in1=bvbc[:, vh * 512:(vh + 1) * 512],
                        op=ALU.add), reads=["bvbc"], writes=[br_] + r_vall(n))
            for n in range(nT):
                P.op("act", lambda e, n=n: e.activation(out=vall[:, n, :], in_=vall[:, n, :], func=AF.Gelu),
                     reads=r_vall(n), writes=r_vall(n))
                P.op("dve", lambda e, n=n: e.bn_stats(out=bnst[:, n, 0:6], in_=vall[:, n, 0:512]),
                     reads=r_vall(n), writes=["bnst"])
                P.op("dve", lambda e, n=n: e.bn_stats(out=bnst[:, n, 6:12], in_=vall[:, n, 512:1024]),
                     reads=r_vall(n), writes=["bnst"])
                P.op("dve", lambda e, n=n: e.bn_aggr(out=bnmv[:, n, 0:2], in_=bnst[:, n, :]),
                     reads=["bnst"], writes=["bnmv"])
            P.op("act", lambda e, nT=nT: e.activation(out=bnmv[:, 0:nT, 2], in_=bnmv[:, 0:nT, 1], func=AF.Sqrt,
                                                      bias=epsT[:], scale=1.0),
                 reads=["bnmv", "epsT"], writes=["bnmv"])
            P.op("dve", lambda e, nT=nT: e.reciprocal(out=bnmv[:, 0:nT, 2], in_=bnmv[:, 0:nT, 2]),
                 reads=["bnmv"], writes=["bnmv"])
            for n in range(nT):
                P.op("dve", lambda e, n=n: e.tensor_scalar(out=vn[:, n, :], in0=vall[:, n, :],
                                                           scalar1=bnmv[:, n, 0:1], scalar2=bnmv[:, n, 2:3],
                                                           op0=ALU.subtract, op1=ALU.mult),
                     reads=r_vall(n) + ["bnmv"], writes=r_vn(n))

            conv_build(0)
            for gp in range(4):
                slu, ru = load_block(w_in_d[8 + gp])
                for gg in range(2):
                    g = 2 * gp + gg
                    if g + 1 < CC:
                        conv_build(g + 1)
                    bku, bru = bank()
                    bks, brs = bank()
                    mm_group(bku[:, :T], [(wv256(slu)[:, k, gg * 128:(gg + 1) * 128], hT[:, k, :T]) for k in range(KC)],
                             reads=[ru] + HALL, bankres=bru)
                    for n in range(nT):
                        mm_group(bks[:, n * 128:(n + 1) * 128], [(vn[:, n, g * 128:(g + 1) * 128], wsb[:, g, :])],
                                 reads=[("wsb", g)] + r_vn(n), bankres=brs)
                    gu, gur = tmp()
                    P.op("act", lambda e, g=g, bku=bku, gu=gu: e.activation(
                        out=gu[:, :T], in_=bku[:, :T], func=AF.Gelu, bias=V("b_u", g), scale=1.0),
                        reads=VEC, writes=[bru, gur])
                    t2, t2r = tmp()
                    for n in range(nT):
                        P.op("dve", lambda e, g=g, n=n, bks=bks, t2=t2: e.scalar_tensor_tensor(
                            out=t2[:, n * 128:(n + 1) * 128], in0=bks[:, n * 128:(n + 1) * 128], scalar=V("slg", g),
                            in1=bfull[:, g, :], op0=ALU.mult, op1=ALU.add),
                            reads=VEC + [("bfull", g)], writes=[brs, t2r])
                    P.op("dve", lambda e, g=g, t2=t2, gu=gu: e.tensor_tensor(out=ub[:, g, :T], in0=t2[:, :T],
                                                                            in1=gu[:, :T], op=ALU.mult),
                         reads=[t2r, gur], writes=[("ub", g)])
                    fill(1)
            fill(CC)

            if halo:
                ada_rest()
            if ti == 1:
                dbg("abuf", abuf[:, :, 0:HALO + T], sum([r_a(j) for j in range(CC)], []))
                dbg("vn", vn[:, 0:nT, :], sum([r_vn(n) for n in range(nT)], []))
                dbg("ub", ub[:, :, :T], [("ub", k) for k in range(CC)])
                dbg("ac", acb[:, :, :T], sum([r_ac(j) for j in range(CC)], []))
            for j in range(CC):
                P.op("act", lambda e, j=j: e.activation(out=sqm[:, j, :T], in_=acb[:, j, :T], func=AF.Identity),
                     reads=r_ac(j), writes=[("sqm", j)])
                P.op("act", lambda e, j=j: e.activation(out=sqm[:, CC + j, :T], in_=acb[:, j, :T], func=AF.Square),
                     reads=r_ac(j), writes=[("sqm", CC + j)])
            bkm, brm = bank()
            bkq, brq = bank()
            mm_group(bkm[:, :T], [(onesC[:], sqm[:, j, :T]) for j in range(CC)],
                     reads=["onesC"] + [("sqm", j) for j in range(CC)], bankres=brm)
            mm_group(bkq[:, :T], [(onesC[:], sqm[:, CC + j, :T]) for j in range(CC)],
                     reads=["onesC"] + [("sqm", CC + j) for j in range(CC)], bankres=brq)
            mean, meanr = longs[1], ("L", 1)
            P.op("act", lambda e: e.activation(out=mean[:, :T], in_=bkm[:, :T], func=AF.Identity),
                 writes=[brm, meanr])
            m2, m2r = longs[2], ("L", 2)
            P.op("dve", lambda e: e.tensor_tensor(out=m2[:, :T], in0=mean[:, :T], in1=mean[:, :T], op=ALU.mult),
                 reads=[meanr], writes=[m2r])
            P.op("dve", lambda e: e.tensor_tensor(out=m2[:, :T], in0=bkq[:, :T], in1=m2[:, :T], op=ALU.subtract),
                 reads=[m2r], writes=[brq, m2r])
            P.op("act", lambda e: e.activation(out=m2[:, :T], in_=m2[:, :T], func=AF.Sqrt, bias=epsT[:], scale=1.0),
                 reads=[m2r, "epsT"], writes=[m2r])
            P.op("dve", lambda e: e.reciprocal(out=m2[:, :T], in_=m2[:, :T]), reads=[m2r], writes=[m2r])
            for j in range(CC):
                lt, ltr = tmp()
                P.op("dve", lambda e, j=j, lt=lt: e.tensor_tensor(out=lt[:, :T], in0=acb[:, j, :T], in1=mean[:, :T],
                                                                  op=ALU.subtract),
                     reads=r_ac(j) + [meanr], writes=[ltr])
                P.op("dve", lambda e, lt=lt: e.tensor_tensor(out=lt[:, :T], in0=lt[:, :T], in1=m2[:, :T], op=ALU.mult),
                     reads=[ltr, m2r], writes=[ltr])
                P.op("act", lambda e, j=j, lt=lt: e.activation(out=a2[:, j, :T], in_=lt[:, :T], func=AF.Silu,
                                                               bias=V("clb", j), scale=V("clg", j)),
                     reads=[ltr] + VEC, writes=r_a2(j))

            cur_blocks = {}
            for j in range(KC):
                if j % 2 == 0:
                    cur_blocks["ga"] = load_block(w_in_d[16 + j // 2])
                    cur_blocks["gb"] = load_block(w_in_d[24 + j // 2])
                    cur_blocks["cs"] = load_pair(w_co_d[j // 2], w_so_d[j // 2])
                (sga_, rga), (sgb_, rgb) = cur_blocks["ga"], cur_blocks["gb"]
                scs, rcs = cur_blocks["cs"]
                csv = scs[:, :].rearrange("p (h k c) -> p h k c", h=2, c=256)
                jj = j % 2
                bkga, brga = bank()
                bkgb, brgb = bank()
                bkyb, bryb = bank()
                bkya, brya = bank()
                mm_group(bkga[:, :T], [(wv256(sga_)[:, k, jj * 128:(jj + 1) * 128], hT[:, k, :T]) for k in range(KC)],
                         reads=[rga] + HALL, bankres=brga)
                mm_group(bkgb[:, :T], [(wv256(sgb_)[:, k, jj * 128:(jj + 1) * 128], hT[:, k, :T]) for k in range(KC)],
                         reads=[rgb] + HALL, bankres=brgb)
                mm_group(bkyb[:, :T], [(csv[:, 1, k, jj * 128:(jj + 1) * 128], ub[:, k, :T]) for k in range(CC)],
                         reads=[rcs] + [("ub", k) for k in range(CC)], bankres=bryb)
                mm_group(bkya[:, :T], [(csv[:, 0, k, jj * 128:(jj + 1) * 128], a2[:, k, :T]) for k in range(CC)],
                         reads=[rcs] + sum([r_a2(k) for k in range(CC)], []), bankres=brya)
                sa, sar = tmp()
                sb_, sbr = tmp()
                P.op("act", lambda e, j=j, bkga=bkga, sa=sa: e.activation(
                    out=sa[:, :T], in_=bkga[:, :T], func=AF.Sigmoid, bias=V("b_ga", j), scale=1.0),
                    reads=VEC, writes=[brga, sar])
                P.op("act", lambda e, j=j, bkgb=bkgb, sb_=sb_: e.activation(
                    out=sb_[:, :T], in_=bkgb[:, :T], func=AF.Sigmoid, bias=V("b_gb", j), scale=1.0),
                    reads=VEC, writes=[brgb, sbr])
                P.op("dve", lambda e, bkyb=bkyb, sb_=sb_: e.tensor_tensor(out=sb_[:, :T], in0=sb_[:, :T], in1=bkyb[:, :T],
                                                                          op=ALU.mult),
                     reads=[sbr], writes=[bryb, sbr])
                P.op("dve", lambda e, bkya=bkya, sa=sa: e.tensor_tensor(out=sa[:, :T], in0=sa[:, :T], in1=bkya[:, :T],
                                                                        op=ALU.mult),
                     reads=[sar], writes=[brya, sar])
                P.op("dve", lambda e, j=j, sa=sa, sb_=sb_: e.tensor_tensor(out=sqm[:, j, :T], in0=sa[:, :T], in1=sb_[:, :T],
                                                                           op=ALU.add),
                     reads=[sar, sbr], writes=[("sqm", j)])

            if ti == 1:
                dbg("a2", a2[:, :, :T], sum([r_a2(j) for j in range(CC)], []))
                dbg("merged", sqm[:, :, :T], [("sqm", k) for k in range(KC)])
            for j in range(KC):
                if j % 2 == 0:
                    swo, rwo = load_block(w_out_d[j // 2])
                jj = j % 2
                bk, br = bank()
                mm_group(bk[:, :T], [(wv256(swo)[:, k, jj * 128:(jj + 1) * 128], sqm[:, k, :T]) for k in range(KC)],
                         reads=[rwo] + [("sqm", k) for k in range(KC)], bankres=br)
                P.op("dve", lambda e, j=j, bk=bk: e.scalar_tensor_tensor(
                    out=xT[:, j, :T], in0=bk[:, :T], scalar=MOD(32 + j), in1=xT[:, j, :T], op0=ALU.mult, op1=ALU.add),
                    reads=VEC + [("x", j)], writes=[br, ("x", j)])

            if ti == 1:
                dbg("x1", xT[:, :, :T], [("x", k) for k in range(KC)])
            rms_to_h(T, G2P, 48)

            def fd(jj, k):
                return V("fdw", jj * 3 + k)

            o_fdw = voff["fdw"][0]
            fdw3 = vecs[:, o_fdw:o_fdw + 2 * FC * 3].rearrange("p (j k) -> p j k", k=3)
            first_own = (ti == 1)
            if halo:
                P.op("dve", lambda e: e.tensor_scalar(out=hhalo[:, :, :], in0=hT[:, :, T - 2:T], scalar1=V("flag", 0),
                                                      scalar2=None, op0=ALU.mult), reads=VEC + HALL, writes=["hh"])
                return
            if first_own:
                bk_small, br_small = bank()
                state["skip"] = int(br_small[1])
            if not first_own:
                P.op("dve", lambda e, cur=cur: e.tensor_tensor(out=cw[:, :, 0], in0=fdw3[:, :, 0], in1=upc[cur][:, :, 0],
                                                               op=ALU.mult), reads=VEC + [("upc", cur)], writes=["cw"])
                P.op("dve", lambda e, cur=cur: e.tensor_tensor(out=cw[:, :, 1], in0=fdw3[:, :, 1], in1=upc[cur][:, :, 1],
                                                               op=ALU.mult), reads=VEC + [("upc", cur)], writes=["cw"])
                P.op("dve", lambda e: e.tensor_tensor(out=cw[:, :, 0], in0=cw[:, :, 0], in1=cw[:, :, 1], op=ALU.add),
                     reads=["cw"], writes=["cw"])
                P.op("dve", lambda e, cur=cur: e.tensor_tensor(out=cw[:, :, 1], in0=fdw3[:, :, 0], in1=upc[cur][:, :, 1],
                                                               op=ALU.mult), reads=VEC + [("upc", cur)], writes=["cw"])
            for jf in range(FC):
                sup, rup = load_block(w_up_d[jf])
                outs = []
                for half, jj in ((0, jf), (1, FC + jf)):
                    if first_own:
                        mm_group(bk_small[:, 2 * jj:2 * jj + 2],
                                 [(wv256(sup)[:, k, half * 128:(half + 1) * 128], hhalo[:, k, :]) for k in range(KC)],
                                 reads=[rup, "hh"], bankres=br_small)
                        P.op("dve", lambda e, jj=jj: e.tensor_scalar(
                            out=cw[:, jj, 0:1], in0=bk_small[:, 2 * jj:2 * jj + 1], scalar1=fd(jj, 0), scalar2=None,
                            op0=ALU.mult), reads=VEC, writes=[br_small, "cw"])
                        P.op("dve", lambda e, jj=jj: e.scalar_tensor_tensor(
                            out=cw[:, jj, 0:1], in0=bk_small[:, 2 * jj + 1:2 * jj + 2], scalar=fd(jj, 1),
                            in1=cw[:, jj, 0:1], op0=ALU.mult, op1=ALU.add), reads=VEC + ["cw"], writes=[br_small, "cw"])
                        P.op("dve", lambda e, jj=jj: e.tensor_scalar(
                            out=cw[:, jj, 1:2], in0=bk_small[:, 2 * jj + 1:2 * jj + 2], scalar1=fd(jj, 0), scalar2=None,
                            op0=ALU.mult), reads=VEC, writes=[br_small, "cw"])
                    bk, br = bank()
                    mm_group(bk[:, :T], [(wv256(sup)[:, k, half * 128:(half + 1) * 128], hT[:, k, :T]) for k in range(KC)],
                             reads=[rup] + HALL, bankres=br)
                    if halo:
                        P.op("dve", lambda e, jj=jj, bk=bk, nxt=nxt: e.tensor_scalar(
                            out=upc[nxt][:, jj, :], in0=bk[:, T - 2:T], scalar1=V("flag", 0), scalar2=None, op0=ALU.mult),
                            reads=VEC, writes=[br, ("upc", nxt)])
                        continue
                    o_, orr = tmp()
                    P.op("act", lambda e, jj=jj, bk=bk, o_=o_: e.activation(
                        out=o_[:, :T], in_=bk[:, :T], func=AF.Identity, bias=V("fdb", jj), scale=fd(jj, 2)),
                        reads=VEC, writes=[br, orr])
                    P.op("dve", lambda e, jj=jj, bk=bk, o_=o_: e.scalar_tensor_tensor(
                        out=o_[:, 1:T], in0=bk[:, 0:T - 1], scalar=fd(jj, 1), in1=o_[:, 1:T], op0=ALU.mult, op1=ALU.add),
                        reads=VEC + [orr], writes=[br, orr])
                    P.op("dve", lambda e, jj=jj, bk=bk, o_=o_: e.scalar_tensor_tensor(
                        out=o_[:, 2:T], in0=bk[:, 0:T - 2], scalar=fd(jj, 0), in1=o_[:, 2:T], op0=ALU.mult, op1=ALU.add),
                        reads=VEC + [orr], writes=[br, orr])
                    P.op("dve", lambda e, jj=jj, o_=o_: e.tensor_tensor(out=o_[:, 0:2], in0=o_[:, 0:2], in1=cw[:, jj, :],
                                                                        op=ALU.add),
                         reads=["cw", orr], writes=[orr])
                    P.op("dve", lambda e, jj=jj, bk=bk, nxt=nxt: e.tensor_copy(out=upc[nxt][:, jj, :], in_=bk[:, T - 2:T]),
                         writes=[br, ("upc", nxt)])
                    outs.append((o_, orr))
                if halo:
                    continue
                (ov, ovr), (og, ogr) = outs
                P.op("act", lambda e, og=og: e.activation(out=og[:, :T], in_=og[:, :T], func=AF.Silu),
                     reads=[ogr], writes=[ogr])
                P.op("dve", lambda e, jf=jf, ov=ov, og=og: e.tensor_tensor(out=gbuf[:, jf, :T], in0=og[:, :T], in1=ov[:, :T],
                                                                           op=ALU.mult),
                     reads=[ovr, ogr], writes=r_g(jf))

            state["skip"] = None
            if ti == 1:
                dbg("g", gbuf[:, :, :T], sum([r_g(k) for k in range(FC)], []))
                dbg("cw", cw[:, :, :], ["cw"])
            if not halo:
                for j in range(KC):
                    sd0, rd0 = load_block(w_dn_d[2 * j], HK * 128)
                    sd1, rd1 = load_block(w_dn_d[2 * j + 1], HK * 128)
                    bk, br = bank()
                    pairs = []
                    for hh, sd in ((0, sd0), (1, sd1)):
                        wv = sd[:, 0:HK * 128].rearrange("p (k c) -> p k c", c=128)
                        for kk in range(HK):
                            pairs.append((wv[:, kk, :], gbuf[:, hh * HK + kk, :T]))
                    mm_group(bk[:, :T], pairs, reads=[rd0, rd1] + sum([r_g(k) for k in range(FC)], []), bankres=br)
                    P.op("dve", lambda e, j=j, bk=bk: e.scalar_tensor_tensor(
                        out=xT[:, j, :T], in0=bk[:, :T], scalar=MOD(80 + j), in1=xT[:, j, :T], op0=ALU.mult, op1=ALU.add),
                        reads=VEC + [("x", j)], writes=[br, ("x", j)])
                if ti == 1:
                    dbg("x2", xT[:, :, :T], [("x", k) for k in range(KC)])
                rstd, rres_ = rms_to_h(T, None, 0)
                for k in range(KC):
                    P.op("dve", lambda e, k=k, rstd=rstd: e.scalar_tensor_tensor(
                        out=ystage[:, k, :T], in0=xT[:, k, :T], scalar=V("fg", k), in1=rstd[:, :T], op0=ALU.mult, op1=ALU.mult),
                        reads=VEC + [("x", k), rres_], writes=rres(k * TM, (k + 1) * TM))
                P.op("sp", lambda e, t0=t0, T=T: e.dma_start(out=yd[:, :, t0 - 128:t0 - 128 + T], in_=ystage[:, :, :T]),
                     reads=rres(0, KC * TM), dma="yout")

        t0_ = 0
        for ti_, T_ in enumerate(cfg.tiles):
            do_tile(ti_, T_, t0_)
            t0_ += T_

        n_out = P.dcnt.get("yout", 0)

        names = P.sem_names()
        sems = {n: st.enter_context(nc.semaphore(n)) for n in names}
        block = st.enter_context(nc.Block())

        @block.tensor
        def _(e):
            P.emit("pe", e, sems)

        @block.scalar
        def _(e):
            P.emit("act", e, sems)

        @block.vector
        def _(e):
            P.emit("dve", e, sems)

        @block.gpsimd
        def _(e):
            P.emit("pool", e, sems)

        @block.sync
        def _(e):
            P.emit("sp", e, sems)
            e.wait_ge(sems["d_yout"], 16 * n_out)
            for i in range(1, state.get("ndbg", 0) + 1):
                e.wait_ge(sems["d_dbg%d" % i], 16)
    return nc


def _blk_k(W, kc, cols):
    K, N = W.shape
    nb = N // cols
    return np.ascontiguousarray(W.reshape(kc, 128, nb, cols).transpose(2, 1, 0, 3).reshape(nb, 128, kc * cols))


def _fm(v, n):
    return np.asarray(v, np.float32).reshape(n, 128).T


def prep_shared(cfg, inp):
    FC, HK, DFF = cfg.FC, cfg.HK, cfg.DFF
    f = lambda a: np.asarray(a, np.float32)
    sh = {}
    sh["w_ada"] = _blk_k(f(inp["w_ada"])[0], KC, 256)
    sh["w_in"] = _blk_k(f(inp["w_in"])[0], KC, 256)
    sh["w_co"] = _blk_k(f(inp["w_conv_out"])[0], CC, 256)
    sh["w_so"] = _blk_k(f(inp["w_sgu_out"])[0], CC, 256)
    sh["w_out"] = _blk_k(f(inp["w_out"])[0], KC, 256)
    wu = f(inp["w_up"])[0]
    wu2 = np.concatenate([wu[:, :DFF].reshape(D, FC, 1, 128), wu[:, DFF:].reshape(D, FC, 1, 128)], axis=2)
    sh["w_up"] = _blk_k(wu2.reshape(D, FC * 256), KC, 256)
    wd = f(inp["w_down"])[0]
    sh["w_dn"] = np.ascontiguousarray(
        wd.reshape(2, HK, 128, KC, 128).transpose(3, 0, 2, 1, 4).reshape(2 * KC, 128, HK * 128))
    b_in = f(inp["b_in"])[0]
    sh["bvbc"] = np.ascontiguousarray(np.broadcast_to(b_in[3072:4096][None, :], (128, 1024)))
    sh["bsp"] = np.ascontiguousarray(np.broadcast_to(f(inp["b_spatial"])[0].reshape(1, 1024), (128, 1024)))
    sh["wspT"] = np.ascontiguousarray(f(inp["w_spatial"])[0].transpose(2, 0, 1).reshape(128, 1024))
    sh["maskT"] = np.ascontiguousarray(np.triu(np.ones((128, 128), np.float32)))
    sh["ident"] = np.ascontiguousarray(np.eye(128, dtype=np.float32))
    voff, NV = vec_layout(cfg)
    vecs = np.zeros((128, NV), np.float32)

    def put(name, arr):
        o, n = voff[name]
        assert arr.shape == (128, n), (name, arr.shape, n)
        vecs[:, o:o + n] = arr

    put("n1g", _fm(f(inp["norm1_g"])[0], KC))
    put("n2g", _fm(f(inp["norm2_g"])[0], KC))
    put("fg", _fm(f(inp["final_g"]), KC))
    put("b_aval", _fm(b_in[0:1024], CC))
    put("b_agate", _fm(b_in[1024:2048], CC))
    put("b_u", _fm(b_in[2048:3072], CC))
    put("b_ga", _fm(b_in[4096:6144], KC))
    put("b_gb", _fm(b_in[6144:8192], KC))
    put("cdw", np.ascontiguousarray(f(inp["conv_dw_w"])[0].reshape(CONV_K, CC, 128).transpose(2, 1, 0)).reshape(128, CC * CONV_K))
    put("cdb", _fm(f(inp["conv_dw_b"])[0], CC))
    put("clg", _fm(f(inp["conv_ln_g"])[0], CC))
    put("clb", _fm(f(inp["conv_ln_b"])[0], CC))
    put("slg", _fm(f(inp["sgu_ln_g"])[0], CC))
    put("slb", _fm(f(inp["sgu_ln_b"])[0], CC))
    put("fdw", np.ascontiguousarray(f(inp["ffn_dw_w"])[0].reshape(3, 2 * FC, 128).transpose(2, 1, 0)).reshape(128, 2 * FC * 3))
    put("fdb", _fm(f(inp["ffn_dw_b"])[0], 2 * FC))
    put("bada", _fm(f(inp["b_ada"])[0], 96))
    sh["vecs"] = vecs
    return sh, voff


def make_in_maps(cfg, inp):
    x = np.asarray(inp["x"], np.float32)
    c = np.asarray(inp["c"], np.float32)
    B, S, _ = x.shape
    per_seq = NCORES // B
    own = cfg.NOUT
    assert per_seq * own == S
    sh, voff = prep_shared(cfg, inp)
    in_maps = []
    for core in range(NCORES):
        b, q = divmod(core, per_seq)
        s0 = q * own
        xc = np.zeros((128 + own, D), np.float32)
        xc[128:] = x[b, s0:s0 + own]
        if q > 0:
            xc[:128] = x[b, s0 - 128:s0]
        xT = np.ascontiguousarray(xc.reshape(128 + own, KC, 128).transpose(2, 1, 0))
        vecs = sh["vecs"].copy()
        o, n = voff["flag"]
        vecs[:, o] = 1.0 if q > 0 else 0.0
        o, n = voff["cT"]
        vecs[:, o:o + n] = _fm(c[b], KC)
        m = {k: v for k, v in sh.items() if k != "vecs"}
        m["vecs"] = vecs
        m["xT"] = xT
        in_maps.append(m)
    return in_maps


def kernel(**inp):
    cfg = CFG
    x = np.asarray(inp["x"], np.float32)
    B, S, _ = x.shape
    per_seq = NCORES // B
    own = cfg.NOUT
    in_maps = make_in_maps(cfg, inp)
    nc = build_program(cfg)
    res = run_bass_kernel_spmd(nc, in_maps, core_ids=list(range(NCORES)))
    global LAST_RES
    LAST_RES = res.results
    out = np.empty((B, S, D), np.float32)
    for core in range(NCORES):
        b, q = divmod(core, per_seq)
        yT = np.asarray(res.results[core]["yT"])
        out[b, q * own:(q + 1) * own] = yT.transpose(2, 1, 0).reshape(own, D)
    return out
```
